# Optimizing a Trainium2 kernel written in Bass

```python
import jax, jax.numpy as jnp
from jax import lax
import numpy as np

D_MODEL = 4096
BATCH = 1
SEQ = 8192
DEPTH = 4

N_Q_HEADS = 16
N_KV_GROUPS = 4
HEAD_DIM = 128
Q_PER_KV = N_Q_HEADS // N_KV_GROUPS
ATTN_WIDTH = N_Q_HEADS * HEAD_DIM
KV_WIDTH = N_KV_GROUPS * HEAD_DIM
N_NSA_BRANCHES = 3
ROT_DIM = HEAD_DIM // 4
ROPE_THETA = 500000.0
CMP_LEN = 32
CMP_STRIDE = 16
SEL_BLOCK = 64
SEL_TOP = 16
N_LOCAL_SEL = 2
WINDOW = 512
Q_BLOCK = 128
FORCE_SCORE = 2.0 * Q_PER_KV + 1.0
RNN_WIDTH = 2048
RNN_BLOCKS = 16
RNN_BLOCK_DIM = RNN_WIDTH // RNN_BLOCKS
RNN_CONV = 4
RG_C = 8.0
D_FF = 2 * D_MODEL
FFN_CONV = 3
COND_RANK = 512
N_MOD = 6
EPS = 1e-6
NEG_INF = -1e30

SPLIT_SIZES = (ATTN_WIDTH, KV_WIDTH, KV_WIDTH, KV_WIDTH, KV_WIDTH, KV_WIDTH, KV_WIDTH,
               N_Q_HEADS * N_NSA_BRANCHES, RNN_WIDTH, RNN_WIDTH, D_MODEL, D_MODEL)
N_IN = ATTN_WIDTH + 6 * KV_WIDTH + N_Q_HEADS * N_NSA_BRANCHES + 2 * RNN_WIDTH + 2 * D_MODEL

kernel_name = 'hybrid_nsa_rglru_convffn_adaln'


def rms_norm(x, gain):
    x32 = x.astype(jnp.float32)
    y = x32 * lax.rsqrt(jnp.mean(x32 * x32, axis=-1, keepdims=True) + EPS)
    return (y * gain.astype(jnp.float32)).astype(x.dtype)


def rope_tables(positions):
    inv_freq = ROPE_THETA ** (-jnp.arange(0, ROT_DIM, 2, dtype=jnp.float32) / ROT_DIM)
    ang = positions.astype(jnp.float32)[..., None] * inv_freq
    return jnp.cos(ang)[:, :, None, :], jnp.sin(ang)[:, :, None, :]


def apply_partial_rope(x, cos, sin):
    xr = x[..., :ROT_DIM].astype(jnp.float32)
    x1, x2 = xr[..., :ROT_DIM // 2], xr[..., ROT_DIM // 2:]
    rot = jnp.concatenate([x1 * cos - x2 * sin, x2 * cos + x1 * sin], axis=-1)
    return jnp.concatenate([rot.astype(x.dtype), x[..., ROT_DIM:]], axis=-1)


def causal_depthwise_conv(x, w, b):
    k_w = w.shape[0]
    s = x.shape[1]
    xp = jnp.pad(x, ((0, 0), (k_w - 1, 0), (0, 0)))
    y = b
    for j in range(k_w):
        y = y + xp[:, j:j + s] * w[j]
    return y


def masked_softmax(s, mask):
    s = jnp.where(mask, s, NEG_INF)
    m = jnp.max(s, axis=-1, keepdims=True)
    p = jnp.exp(s - m) * mask
    return p / jnp.maximum(jnp.sum(p, axis=-1, keepdims=True), 1e-30)


def linear_scan(a, b):
    def combine(l, r):
        a_l, b_l = l
        a_r, b_r = r
        return a_l * a_r, a_r * b_l + b_r
    _, h = lax.associative_scan(combine, (a, b), axis=1)
    return h


def compress(k, pe, w):
    s = k.shape[1]
    n_cmp = (s - CMP_LEN) // CMP_STRIDE + 1
    idx = jnp.arange(n_cmp)[:, None] * CMP_STRIDE + jnp.arange(CMP_LEN)[None, :]
    kb = k[:, idx] + pe[None, None, :, None, :]
    return jnp.einsum('bnlgd,lde->bgne', kb, w)


def nsa_attention(q, k_cmp, v_cmp, k_sel, v_sel, k_win, v_win, gates):
    b, s = q.shape[0], q.shape[1]
    n_cmp = k_cmp.shape[2]
    n_sel = s // SEL_BLOCK
    n_top = min(SEL_TOP, n_sel)
    n_qblk = s // Q_BLOCK
    scale = HEAD_DIM ** -0.5
    q_g = q.reshape(b, s, N_KV_GROUPS, Q_PER_KV, HEAD_DIM).transpose(0, 2, 3, 1, 4)
    g_g = gates.reshape(b, s, N_KV_GROUPS, Q_PER_KV, N_NSA_BRANCHES).transpose(0, 2, 3, 1, 4)
    ks_b = k_sel.transpose(0, 2, 1, 3).reshape(b, N_KV_GROUPS, n_sel, SEL_BLOCK, HEAD_DIM)
    vs_b = v_sel.transpose(0, 2, 1, 3).reshape(b, N_KV_GROUPS, n_sel, SEL_BLOCK, HEAD_DIM)
    pad = ((0, 0), (0, 0), (WINDOW, 0), (0, 0))
    kw_p = jnp.pad(k_win.transpose(0, 2, 1, 3), pad)
    vw_p = jnp.pad(v_win.transpose(0, 2, 1, 3), pad)
    cmp_start = np.arange(n_cmp) * CMP_STRIDE
    sel_start = np.arange(n_sel) * SEL_BLOCK
    ov = np.clip(np.minimum(cmp_start[:, None] + CMP_LEN, sel_start[None, :] + SEL_BLOCK)
                 - np.maximum(cmp_start[:, None], sel_start[None, :]), 0, None) / CMP_LEN
    overlap = jnp.asarray(ov.astype(np.float32))
    cmp_end = jnp.asarray((cmp_start + CMP_LEN - 1).astype(np.int32))
    sel_ids = jnp.arange(n_sel)
    b_ids = jnp.arange(b)[:, None, None, None]
    g_ids = jnp.arange(N_KV_GROUPS)[None, :, None, None]
    win_off = jnp.arange(Q_BLOCK + WINDOW)
    sel_off = jnp.arange(SEL_BLOCK)

    def block(qb):
        start = qb * Q_BLOCK
        t = start + jnp.arange(Q_BLOCK)
        qblk = lax.dynamic_slice_in_dim(q_g, start, Q_BLOCK, axis=3)
        gblk = lax.dynamic_slice_in_dim(g_g, start, Q_BLOCK, axis=3)
        s_c = jnp.einsum('bgzqd,bgnd->bgzqn', qblk, k_cmp).astype(jnp.float32) * scale
        p_c = masked_softmax(s_c, cmp_end[None, :] <= t[:, None])
        o_c = jnp.einsum('bgzqn,bgnd->bgzqd', p_c.astype(v_cmp.dtype), v_cmp)
        imp = jnp.einsum('bgzqn,nj->bgqj', p_c, overlap)
        cur = (t // SEL_BLOCK)[:, None]
        valid = sel_ids[None, :] <= cur
        forced = (sel_ids[None, :] == 0) | (valid & (sel_ids[None, :] > cur - N_LOCAL_SEL))
        score = jnp.where(forced, FORCE_SCORE, jnp.where(valid, imp, -1.0))
        _, idx = lax.top_k(score, n_top)
        kg = ks_b[b_ids, g_ids, idx].reshape(b, N_KV_GROUPS, Q_BLOCK, n_top * SEL_BLOCK, HEAD_DIM)
        vg = vs_b[b_ids, g_ids, idx].reshape(b, N_KV_GROUPS, Q_BLOCK, n_top * SEL_BLOCK, HEAD_DIM)
        pos_s = (idx[..., None] * SEL_BLOCK + sel_off).reshape(b, N_KV_GROUPS, Q_BLOCK, n_top * SEL_BLOCK)
        mask_s = (pos_s <= t[None, None, :, None])[:, :, None]
        s_s = jnp.einsum('bgzqd,bgqkd->bgzqk', qblk, kg).astype(jnp.float32) * scale
        p_s = masked_softmax(s_s, mask_s)
        o_s = jnp.einsum('bgzqk,bgqkd->bgzqd', p_s.astype(vg.dtype), vg)
        kw = lax.dynamic_slice_in_dim(kw_p, start, Q_BLOCK + WINDOW, axis=2)
        vw = lax.dynamic_slice_in_dim(vw_p, start, Q_BLOCK + WINDOW, axis=2)
        pos_w = (start - WINDOW + win_off)[None, :]
        mask_w = (pos_w <= t[:, None]) & (pos_w > t[:, None] - WINDOW) & (pos_w >= 0)
        s_w = jnp.einsum('bgzqd,bgkd->bgzqk', qblk, kw).astype(jnp.float32) * scale
        p_w = masked_softmax(s_w, mask_w)
        o_w = jnp.einsum('bgzqk,bgkd->bgzqd', p_w.astype(vw.dtype), vw)
        o = gblk[..., 0:1] * o_c + gblk[..., 1:2] * o_s + gblk[..., 2:3] * o_w
        return o.transpose(0, 3, 1, 2, 4).reshape(b, Q_BLOCK, ATTN_WIDTH)

    out = lax.map(block, jnp.arange(n_qblk))
    return out.transpose(1, 0, 2, 3).reshape(b, s, ATTN_WIDTH)


def rg_lru(x, w_a, b_a, w_x, b_x, lam):
    b, s, _ = x.shape
    xb = x.reshape(b, s, RNN_BLOCKS, RNN_BLOCK_DIM)
    r = jax.nn.sigmoid(jnp.einsum('bshi,hij->bshj', xb, w_a).reshape(b, s, RNN_WIDTH) + b_a)
    i = jax.nn.sigmoid(jnp.einsum('bshi,hij->bshj', xb, w_x).reshape(b, s, RNN_WIDTH) + b_x)
    log_a = -RG_C * r.astype(jnp.float32) * jax.nn.softplus(-lam.astype(jnp.float32))
    a = jnp.exp(log_a)
    bt = jnp.sqrt(-jnp.expm1(2.0 * log_a)) * (i * x).astype(jnp.float32)
    return linear_scan(a, bt).astype(x.dtype)


def hybrid_mixer(u, cos, sin, w_in, q_norm, k_norm, cmp_pe_k, cmp_w_k, cmp_pe_v, cmp_w_v,
                 rnn_conv_w, rnn_conv_b, rg_w_a, rg_b_a, rg_w_x, rg_b_x, rg_lambda,
                 w_attn_up, w_rnn_up, w_out):
    b, s, _ = u.shape
    points = [int(p) for p in np.cumsum(SPLIT_SIZES)[:-1]]
    (q, kc, vc, ks_, vs_, kw, vw, g_nsa, rx, ry, g_attn, g_rnn) = jnp.split(u @ w_in, points, axis=-1)

    def heads(t, n):
        return t.reshape(b, s, n, HEAD_DIM)

    q = apply_partial_rope(rms_norm(heads(q, N_Q_HEADS), q_norm), cos, sin)
    kc = rms_norm(compress(apply_partial_rope(heads(kc, N_KV_GROUPS), cos, sin), cmp_pe_k, cmp_w_k), k_norm[0])
    vc = compress(heads(vc, N_KV_GROUPS), cmp_pe_v, cmp_w_v)
    ks_ = apply_partial_rope(rms_norm(heads(ks_, N_KV_GROUPS), k_norm[1]), cos, sin)
    kw = apply_partial_rope(rms_norm(heads(kw, N_KV_GROUPS), k_norm[2]), cos, sin)
    gates = jax.nn.sigmoid(g_nsa).reshape(b, s, N_Q_HEADS, N_NSA_BRANCHES)
    attn = nsa_attention(q, kc, vc, ks_, heads(vs_, N_KV_GROUPS), kw, heads(vw, N_KV_GROUPS), gates)
    xr = causal_depthwise_conv(rx, rnn_conv_w, rnn_conv_b)
    rnn = rg_lru(xr, rg_w_a, rg_b_a, rg_w_x, rg_b_x, rg_lambda) * jax.nn.gelu(ry, approximate=True)
    merged = jax.nn.sigmoid(g_attn) * (attn @ w_attn_up) + jax.nn.sigmoid(g_rnn) * (rnn @ w_rnn_up)
    return merged @ w_out


def conv_ffn(u, w_ffn_in, conv_w, conv_b, w_down):
    gate, up = jnp.split(u @ w_ffn_in, 2, axis=-1)
    gate = causal_depthwise_conv(gate, conv_w, conv_b)
    return (jax.nn.silu(gate) * up) @ w_down


def setup_inputs(seed: int = 0) -> dict:
    key = jax.random.key(seed)
    ks = jax.random.split(key, 32)
    f32 = jnp.float32
    L = DEPTH

    def nrm(k, shape, scale):
        return jax.random.normal(k, shape, f32) * scale

    lam_u = jax.random.uniform(ks[20], (L, RNN_WIDTH), f32, 0.9, 0.999)
    a_base = lam_u ** (1.0 / RG_C)
    rg_lambda = jnp.log(a_base) - jnp.log1p(-a_base)
    return {
        'x': nrm(ks[0], (BATCH, SEQ, D_MODEL), 1.0),
        'c': nrm(ks[1], (BATCH, D_MODEL), 1.0),
        'positions': jnp.broadcast_to(jnp.arange(SEQ, dtype=jnp.int32), (BATCH, SEQ)),
        'w_cond': nrm(ks[2], (D_MODEL, COND_RANK), D_MODEL ** -0.5),
        'b_cond': nrm(ks[3], (COND_RANK,), 0.01),
        'w_mod': nrm(ks[4], (L, COND_RANK, N_MOD * D_MODEL), 0.5 * COND_RANK ** -0.5),
        'b_mod': nrm(ks[5], (L, N_MOD * D_MODEL), 0.01),
        'norm_mix': 1.0 + nrm(ks[6], (L, D_MODEL), 0.02),
        'norm_ffn': 1.0 + nrm(ks[7], (L, D_MODEL), 0.02),
        'w_in': nrm(ks[8], (L, D_MODEL, N_IN), D_MODEL ** -0.5),
        'q_norm': 1.0 + nrm(ks[9], (L, HEAD_DIM), 0.02),
        'k_norm': 1.0 + nrm(ks[10], (L, N_NSA_BRANCHES, HEAD_DIM), 0.02),
        'cmp_pe_k': nrm(ks[11], (L, CMP_LEN, HEAD_DIM), 0.1),
        'cmp_w_k': nrm(ks[12], (L, CMP_LEN, HEAD_DIM, HEAD_DIM), (CMP_LEN * HEAD_DIM) ** -0.5),
        'cmp_pe_v': nrm(ks[13], (L, CMP_LEN, HEAD_DIM), 0.1),
        'cmp_w_v': nrm(ks[14], (L, CMP_LEN, HEAD_DIM, HEAD_DIM), (CMP_LEN * HEAD_DIM) ** -0.5),
        'rnn_conv_w': nrm(ks[15], (L, RNN_CONV, RNN_WIDTH), RNN_CONV ** -0.5),
        'rnn_conv_b': nrm(ks[16], (L, RNN_WIDTH), 0.01),
        'rg_w_a': nrm(ks[17], (L, RNN_BLOCKS, RNN_BLOCK_DIM, RNN_BLOCK_DIM), RNN_BLOCK_DIM ** -0.5),
        'rg_b_a': nrm(ks[18], (L, RNN_WIDTH), 0.01),
        'rg_w_x': nrm(ks[19], (L, RNN_BLOCKS, RNN_BLOCK_DIM, RNN_BLOCK_DIM), RNN_BLOCK_DIM ** -0.5),
        'rg_b_x': nrm(ks[21], (L, RNN_WIDTH), 0.01),
        'rg_lambda': rg_lambda,
        'w_attn_up': nrm(ks[22], (L, ATTN_WIDTH, D_MODEL), ATTN_WIDTH ** -0.5),
        'w_rnn_up': nrm(ks[23], (L, RNN_WIDTH, D_MODEL), RNN_WIDTH ** -0.5),
        'w_out': nrm(ks[24], (L, D_MODEL, D_MODEL), D_MODEL ** -0.5),
        'w_ffn_in': nrm(ks[25], (L, D_MODEL, 2 * D_FF), D_MODEL ** -0.5),
        'ffn_conv_w': nrm(ks[26], (L, FFN_CONV, D_FF), FFN_CONV ** -0.5),
        'ffn_conv_b': nrm(ks[27], (L, D_FF), 0.01),
        'w_ffn_down': nrm(ks[28], (L, D_FF, D_MODEL), D_FF ** -0.5),
    }


def reference(x, c, positions, w_cond, b_cond, w_mod, b_mod, norm_mix, norm_ffn, w_in, q_norm, k_norm,
              cmp_pe_k, cmp_w_k, cmp_pe_v, cmp_w_v, rnn_conv_w, rnn_conv_b, rg_w_a, rg_b_a, rg_w_x,
              rg_b_x, rg_lambda, w_attn_up, w_rnn_up, w_out, w_ffn_in, ffn_conv_w, ffn_conv_b, w_ffn_down):
    cos, sin = rope_tables(positions)
    c_emb = jax.nn.silu(c @ w_cond + b_cond)
    h = x
    for l in range(DEPTH):
        mod = c_emb @ w_mod[l] + b_mod[l]
        sh1, sc1, g1, sh2, sc2, g2 = jnp.split(mod[:, None, :], N_MOD, axis=-1)
        u = rms_norm(h, norm_mix[l]) * (1.0 + sc1) + sh1
        h = h + g1 * hybrid_mixer(u, cos, sin, w_in[l], q_norm[l], k_norm[l], cmp_pe_k[l], cmp_w_k[l],
                                  cmp_pe_v[l], cmp_w_v[l], rnn_conv_w[l], rnn_conv_b[l], rg_w_a[l], rg_b_a[l],
                                  rg_w_x[l], rg_b_x[l], rg_lambda[l], w_attn_up[l], w_rnn_up[l], w_out[l])
        u = rms_norm(h, norm_ffn[l]) * (1.0 + sc2) + sh2
        h = h + g2 * conv_ffn(u, w_ffn_in[l], ffn_conv_w[l], ffn_conv_b[l], w_ffn_down[l])
    return h
```

```python
import contextlib
import math
import os
import numpy as np
import concourse.bass as bass
import concourse.mybir as mybir
from concourse.bass_utils import run_bass_kernel_spmd

F32 = mybir.dt.float32
BF16 = mybir.dt.bfloat16
I32 = mybir.dt.int32
AF = mybir.ActivationFunctionType
ALU = mybir.AluOpType
AX = mybir.AxisListType

NCORES = 8
D = 4096
KC = D // 128
NIN = 17456
DFF = 8192
EPS = 1e-6
BIG = 30000.0
SEG_Q, SEG_KC, SEG_VC, SEG_KS, SEG_VS, SEG_KW, SEG_VW, SEG_GN, SEG_RX, SEG_RY, SEG_GA, SEG_GR = (
    0, 2048, 2560, 3072, 3584, 4096, 4608, 5120, 5168, 7216, 9264, 13360)
V_NMIX, V_NFFN, V_QN, V_KN, V_RCW, V_RCB, V_RBA, V_RBX, V_RLAM, V_FCW, V_FCB = (
    0, 32, 64, 65, 68, 132, 148, 164, 180, 196, 388)
NV = 452


class Buf:
    __slots__ = ("ap", "w", "r", "ps")

    def __init__(self, ap=None):
        self.ap = ap
        self.w = {}
        self.r = {}
        self.ps = False

    def __getitem__(self, k):
        return self.ap[k]


class Trk:
    NDMA = 4

    def __init__(self, nc):
        self.nc = nc
        self.eng = {"pe": nc.tensor, "act": nc.scalar, "dve": nc.vector, "pool": nc.gpsimd, "sp": nc.sync}
        self.sems = {}
        self.val = {}
        self.waited = {e: {} for e in self.eng}
        for e in ("pe", "act", "dve", "pool"):
            self._mk(e)
        self.dq = {}
        for q in ("sp", "pool"):
            keys = []
            for i in range(self.NDMA):
                k = "d_%s_%d" % (q, i)
                self._mk(k)
                keys.append(k)
            self.dq[q] = [keys, 0]
        self._mk("cc")
        self.n_inst = 0
        self.pslock = Buf() if os.environ.get("KPSLOCK", "1") == "1" else None

    def _mk(self, key):
        self.sems[key] = self.nc.alloc_semaphore(name="s_" + key)
        self.val[key] = 0

    def _wait(self, e, key, val):
        if val <= 0 or self.waited[e].get(key, 0) >= val:
            return
        self.eng[e].wait_ge(self.sems[key], val)
        self.waited[e][key] = val
        self.n_inst += 1

    def _deps(self, e, reads, writes, waw, is_dma):
        own = None if is_dma else e
        for b in reads:
            for k, v in b.w.items():
                self._wait(e, k, v)
        for b in writes:
            if b.r:
                for k, v in b.r.items():
                    if k != own:
                        self._wait(e, k, v)
                if waw:
                    for k, v in b.w.items():
                        if k != own:
                            self._wait(e, k, v)
                b.w = {}
                b.r = {}
            elif waw:
                for k, v in b.w.items():
                    if k != own:
                        self._wait(e, k, v)

    def _commit(self, key, val, reads, writes):
        for b in reads:
            if b.r.get(key, 0) < val:
                b.r[key] = val
        for b in writes:
            if b.w.get(key, 0) < val:
                b.w[key] = val

    def op(self, e, fn, reads=(), writes=(), waw=True):
        if self.pslock is not None:
            if e == "pe":
                if any(b.ps for b in writes):
                    self._deps(e, (), (self.pslock,), False, False)
                    writes = list(writes) + [self.pslock]
            elif any(b.ps for b in reads):
                reads = list(reads) + [self.pslock]
        self._deps(e, reads, writes, waw, False)
        ins = fn(self.eng[e])
        self.val[e] += 1
        ins.then_inc(self.sems[e], 1)
        self._commit(e, self.val[e], reads, writes)
        self.n_inst += 1
        return ins

    def dma(self, q, out, in_, reads=(), writes=(), waw=False, **kw):
        keys, idx = self.dq[q]
        k = keys[idx % len(keys)]
        self.dq[q][1] = idx + 1
        self._wait(q, k, self.val[k])
        self._deps(q, reads, writes, waw, True)
        ins = self.eng[q].dma_start(out=out, in_=in_, **kw)
        self.val[k] += 16
        ins.then_inc(self.sems[k], 16)
        self._commit(k, self.val[k], reads, writes)
        self.n_inst += 1
        return ins

    def allgather(self, out_ap, in_ap, reads=(), writes=()):
        e = "pool"
        self._wait(e, "cc", self.val["cc"])
        self._deps(e, reads, writes, True, True)
        ins = self.nc.gpsimd.collective_compute("AllGather", ALU.bypass, replica_groups=[list(range(NCORES))],
                                               ins=[in_ap], outs=[out_ap])
        self.val["cc"] += 1
        ins.then_inc(self.sems["cc"])
        self._commit("cc", self.val["cc"], reads, writes)
        self.n_inst += 1
        return ins

    def barrier(self):
        for e in self.eng:
            for k, v in self.val.items():
                self._wait(e, k, v)


class Builder:
    def __init__(self, S, DEPTH, dbg=()):
        self.S, self.DEPTH, self.dbg = S, DEPTH, set(dbg)
        self.T = S // NCORES
        self.NQ = self.T // 128
        self.CH = min(512, self.T)
        self.NCH = self.T // self.CH
        self.NKT = S // 128
        self.NCMP = (S - 32) // 16 + 1
        self.NCT = (self.NCMP + 127) // 128
        self.NSEL = S // 64
        self.nc = bass.Bass("TRN2", target_bir_lowering=False)
        self.tk = Trk(self.nc)
        self.ins = {}
        self.outs = {}
        self.psi = 0

    def din(self, name, shape, dt=F32):
        b = Buf(self.nc.dram_tensor(name, list(shape), dt, kind="ExternalInput").ap())
        self.ins[name] = (tuple(shape), dt)
        return b

    def dout(self, name, shape, dt=F32):
        b = Buf(self.nc.dram_tensor(name, list(shape), dt, kind="ExternalOutput").ap())
        self.outs[name] = (tuple(shape), dt)
        return b

    def dint(self, name, shape, dt):
        return Buf(self.nc.dram_tensor(name, list(shape), dt, kind="Internal").ap())

    def sb(self, es, name, shape, dt=F32):
        self.uid = getattr(self, "uid", 0) + 1
        return Buf(es.enter_context(self.nc.sbuf_tensor("sb%d_%s" % (self.uid, name), list(shape), dt)))

    def pool(self, es, name, shape, dt, n):
        bufs = [self.sb(es, "%s%d" % (name, i), shape, dt) for i in range(n)]
        state = [0]

        def nxt():
            b = bufs[state[0] % n]
            state[0] += 1
            return b
        return nxt

    def ps(self):
        rot = getattr(self, "ps_rot", None) or self.psum
        b = rot[self.psi % len(rot)]
        self.psi += 1
        return b

    def tap(self, name, src_buf, src_ap, shape, dt):
        if name not in self.dbg:
            return
        o = self.dout("dbg_" + name, shape, F32)
        self.tk.dma("pool", o.ap, src_ap, reads=[src_buf], writes=[o])

    def build(self):
        nc, tk = self.nc, self.tk
        S, T, NQ, CH, NCH, DEPTH = self.S, self.T, self.NQ, self.CH, self.NCH, self.DEPTH
        with contextlib.ExitStack() as top:
            self.pid = nc.sync.snap(nc.sync.partition_id(), min_val=0, max_val=NCORES - 1)
            self.pidp = nc.gpsimd.snap(nc.gpsimd.partition_id(), min_val=0, max_val=NCORES - 1)
            self.psum = [Buf(top.enter_context(nc.psum_tensor("ps%d" % i, [128, 512], F32))) for i in range(7)]
            self.psT = Buf(top.enter_context(nc.psum_tensor("psT", [128, 1024], BF16)))
            for b_ in self.psum + [self.psT]:
                b_.ps = True
            X = self.din("x", [T, D])
            POS = self.din("pos", [1, T], I32)
            CT = self.din("cT", [128, KC])
            WCOND = self.din("w_cond", [D, 512])
            BCONDT = self.din("b_condT", [128, 4])
            WMOD = self.din("w_mod", [DEPTH * 512, 3072])
            BMODT = self.din("b_modT", [128, DEPTH * 24])
            VECS = self.din("vecs", [128, DEPTH * NV])
            CONST = self.din("consts", [128, 4 * 128 + 2])
            self.W = {}
            wspec = dict(w_in=(D, NIN), w_ffn_in=(D, 2 * DFF), w_ffn_down=(DFF, D), w_out=(D, D),
                         w_attn_up=(2048, D), w_rnn_up=(2048, D))
            for nm, (r, c) in wspec.items():
                self.W[nm] = self.din(nm, [DEPTH * (r // NCORES), c])
            Y = self.dout("y", [T, D])
            self.HT = self.dint("hT", [D, T], F32)
            self.HTc = [Buf(self.HT.ap) for _ in range(NCH)]
            self.WB = {}
            for nm, (r, c) in wspec.items():
                self.WB[nm] = [(self.dint("%s_loc%d" % (nm, l), [r // NCORES, c], BF16),
                                self.dint("%s_full%d" % (nm, l), [r, c], BF16)) for l in range(DEPTH)]
            cst = self.sb(top, "cst", [128, 4 * 128 + 2])
            self.ident = cst
            tk.dma("sp", cst[:], CONST[:, :], reads=[CONST], writes=[cst])
            self.cst = cst
            self.epsb = self.sb(top, "epsb", [128, 1])
            tk.op("pool", lambda e: e.memset(self.epsb[:], EPS), writes=[self.epsb])
            self.oneb = self.sb(top, "oneb", [128, 1])
            tk.op("pool", lambda e: e.memset(self.oneb[:], 1.0), writes=[self.oneb])
            self.kb = self.sb(top, "kb", [128, 8])
            for k in (2, 3, 4, 5):
                tk.op("pool", lambda e, k=k: e.memset(self.kb[:, k:k + 1], 1.0 / k), writes=[self.kb], waw=False)
            ident_f = lambda: cst[:, 0:128]
            ones_f = lambda: cst[:, 128:256]
            rmat_f = lambda: cst[:, 256:384]
            invf = lambda: cst[:, 512:513]
            self.identb = self.sb(top, "identb", [128, 128], BF16)
            tk.op("dve", lambda e: e.tensor_copy(out=self.identb[:], in_=cst[:, 0:128]), reads=[cst], writes=[self.identb])
            vecs = self.sb(top, "vecs", [128, DEPTH * NV])
            tk.dma("sp", vecs[:], VECS[:, :], reads=[VECS], writes=[vecs])
            self.vecs = vecs
            Ctab = self.sb(top, "Ctab", [128, T])
            Stab = self.sb(top, "Stab", [128, T])
            modall = self.sb(top, "modall", [128, NCORES, DEPTH * 24])

            for l in range(DEPTH):
                for nm, (r, c) in wspec.items():
                    rs = r // NCORES
                    loc, full = self.WB[nm][l]
                    src = self.W[nm]
                    for r0 in range(0, rs, 128):
                        for c0 in range(0, c, 4096):
                            c1 = min(c, c0 + 4096)
                            tk.dma("pool", loc[r0:r0 + 128, c0:c1], src[l * rs + r0:l * rs + r0 + 128, c0:c1],
                                   reads=[src], writes=[loc])
                    tk.allgather(full.ap.opt(), loc.ap.opt(), reads=[loc], writes=[full])

            with contextlib.ExitStack() as es:
                xpool = self.pool(es, "xt", [128, D], F32, 2)
                stpool = self.pool(es, "xst", [128, 512], F32, 3)
                HTv = self.HT.ap.rearrange("(c p) t -> p c t", p=128)
                for tt in range(NQ):
                    xt = xpool()
                    tk.dma("sp", xt[:], X[tt * 128:(tt + 1) * 128, :], reads=[X], writes=[xt])
                    for g4 in range(KC // 4):
                        ps = self.ps()
                        for j in range(4):
                            dc = g4 * 4 + j
                            tk.op("pe", lambda e, ps=ps, j=j, dc=dc, xt=xt: e.transpose(
                                out=ps[:, j * 128:(j + 1) * 128], in_=xt[:, dc * 128:(dc + 1) * 128], identity=ident_f()),
                                reads=[xt, cst], writes=[ps], waw=(j == 0))
                        st = stpool()
                        tk.op("act", lambda e, st=st, ps=ps: e.activation(out=st[:], in_=ps[:], func=AF.Copy),
                              reads=[ps], writes=[st])
                        hb = self.HTc[(tt * 128) // CH]
                        tk.dma("pool", HTv[:, g4 * 4:(g4 + 1) * 4, tt * 128:(tt + 1) * 128],
                               st[:].rearrange("p (c t) -> p c t", c=4), reads=[st], writes=[hb])
                tk.barrier()
            self.tap("hT0", self.HTc[0], self.HT.ap, [D, T], F32)

            MODLOC = self.dint("modloc", [128, DEPTH * 24], F32)
            MODAG = self.dint("modag", [NCORES * 128, DEPTH * 24], F32)
            with contextlib.ExitStack() as es:
                cT = self.sb(es, "cT", [128, KC])
                bcT = self.sb(es, "bcT", [128, 4])
                bmT = self.sb(es, "bmT", [128, DEPTH * 24])
                cemb = self.sb(es, "cemb", [128, 4])
                modl = self.sb(es, "modl", [128, DEPTH * 24])
                wc = self.sb(es, "wc", [128, KC, 512])
                wm = self.sb(es, "wm", [128, 4, 3072])
                tk.dma("sp", cT[:], CT[:, :], reads=[CT], writes=[cT])
                tk.dma("sp", bcT[:], BCONDT[:, :], reads=[BCONDT], writes=[bcT])
                tk.dma("sp", bmT[:], BMODT[:, :], reads=[BMODT], writes=[bmT])
                wcv = WCOND.ap.rearrange("(kc p) n -> p kc n", p=128)
                for q4 in range(4):
                    tk.dma("sp", wc[:, q4 * 8:(q4 + 1) * 8, :], wcv[:, q4 * 8:(q4 + 1) * 8, :], reads=[WCOND], writes=[wc])
                ps = self.ps()
                for m in range(4):
                    for kc in range(KC):
                        tk.op("pe", lambda e, m=m, kc=kc, ps=ps: e.matmul(
                            ps[:, m:m + 1], lhsT=wc[:, kc, m * 128:(m + 1) * 128], rhs=cT[:, kc:kc + 1],
                            start=(kc == 0), stop=(kc == KC - 1)), reads=[wc, cT], writes=[ps], waw=(m == 0 and kc == 0))
                for m in range(4):
                    tk.op("act", lambda e, m=m, ps=ps: e.activation(out=cemb[:, m:m + 1], in_=ps[:, m:m + 1], func=AF.Silu,
                                                                     bias=bcT[:, m:m + 1]), reads=[ps, bcT], writes=[cemb], waw=False)
                for l in range(DEPTH):
                    wmv = WMOD.ap[l * 512:(l + 1) * 512, :].rearrange("(m p) n -> p m n", p=128)
                    tk.dma("sp", wm[:], wmv, reads=[WMOD], writes=[wm])
                    ps = self.ps()
                    for j in range(24):
                        for m in range(4):
                            tk.op("pe", lambda e, m=m, j=j, ps=ps: e.matmul(
                                ps[:, j:j + 1], lhsT=wm[:, m, j * 128:(j + 1) * 128], rhs=cemb[:, m:m + 1],
                                start=(m == 0), stop=(m == 3)), reads=[wm, cemb], writes=[ps], waw=(j == 0 and m == 0))
                    tk.op("dve", lambda e, l=l, ps=ps: e.tensor_tensor(out=modl[:, l * 24:(l + 1) * 24], in0=ps[:, 0:24],
                                                                      in1=bmT[:, l * 24:(l + 1) * 24], op=ALU.add),
                          reads=[ps, bmT], writes=[modl], waw=False)
                tk.dma("sp", MODLOC[:, :], modl[:], reads=[modl], writes=[MODLOC])
                tk.allgather(MODAG.ap.opt(), MODLOC.ap.opt(), reads=[MODLOC], writes=[MODAG])
                tk.dma("sp", modall[:], MODAG.ap.rearrange("(r p) n -> p r n", p=128), reads=[MODAG], writes=[modall])
                tk.barrier()
            self.tap("modag", MODAG, MODAG.ap, [NCORES * 128, DEPTH * 24], F32)

            modc = self.sb(top, "modc", [128, DEPTH, 192])
            gv = self.sb(top, "gv", [128, DEPTH, 64])
            for l in range(DEPTH):
                for r in range(NCORES):
                    tk.op("dve", lambda e, l=l, r=r: e.tensor_copy(out=modc[:, l, r * 24:(r + 1) * 24], in_=modall[:, r, l * 24:(l + 1) * 24]),
                          reads=[modall], writes=[modc], waw=False)
                vb = l * NV
                tk.op("dve", lambda e, l=l, vb=vb: e.scalar_tensor_tensor(out=gv[:, l, 0:32], in0=modc[:, l, 32:64], scalar=1.0,
                                                                         in1=vecs[:, vb + V_NMIX:vb + V_NMIX + 32], op0=ALU.add, op1=ALU.mult),
                      reads=[modc, vecs], writes=[gv], waw=False)
                tk.op("dve", lambda e, l=l, vb=vb: e.scalar_tensor_tensor(out=gv[:, l, 32:64], in0=modc[:, l, 128:160], scalar=1.0,
                                                                         in1=vecs[:, vb + V_NFFN:vb + V_NFFN + 32], op0=ALU.add, op1=ALU.mult),
                      reads=[modc, vecs], writes=[gv], waw=False)
            self.modc, self.gv = modc, gv
            modvec = None

            with contextlib.ExitStack() as es:
                posi = self.sb(es, "posi", [128, T], I32)
                ang = self.sb(es, "ang", [128, T])
                ki = self.sb(es, "ki", [128, T], I32)
                kf = self.sb(es, "kf", [128, T])
                r1 = self.sb(es, "r1", [128, T])
                m1 = self.sb(es, "m1", [128, T])
                tk.dma("sp", posi[:], POS.ap.partition_broadcast(128), reads=[POS], writes=[posi])
                tk.op("dve", lambda e: e.tensor_copy(out=ang[:], in_=posi[:]), reads=[posi], writes=[ang])
                tk.op("dve", lambda e: e.tensor_scalar(out=ang[:], in0=ang[:], scalar1=invf(), scalar2=None, op0=ALU.mult),
                      reads=[ang, cst], writes=[ang])
                TWO_PI = 2.0 * math.pi
                C1 = 6.28125
                C2 = TWO_PI - C1

                def wrap(src, dst):
                    tk.op("dve", lambda e: e.tensor_scalar(out=ki[:], in0=src[:], scalar1=1.0 / TWO_PI, scalar2=None, op0=ALU.mult),
                          reads=[src], writes=[ki])
                    tk.op("dve", lambda e: e.tensor_copy(out=kf[:], in_=ki[:]), reads=[ki], writes=[kf])
                    tk.op("dve", lambda e: e.scalar_tensor_tensor(out=dst[:], in0=kf[:], scalar=-C1, in1=src[:], op0=ALU.mult, op1=ALU.add),
                          reads=[kf, src], writes=[dst])
                    tk.op("dve", lambda e: e.scalar_tensor_tensor(out=dst[:], in0=kf[:], scalar=-C2, in1=dst[:], op0=ALU.mult, op1=ALU.add),
                          reads=[kf, dst], writes=[dst])
                    tk.op("dve", lambda e: e.tensor_scalar(out=m1[:], in0=dst[:], scalar1=math.pi, scalar2=TWO_PI, op0=ALU.is_gt, op1=ALU.mult),
                          reads=[dst], writes=[m1])
                    tk.op("dve", lambda e: e.tensor_tensor(out=dst[:], in0=dst[:], in1=m1[:], op=ALU.subtract), reads=[dst, m1], writes=[dst])
                    tk.op("dve", lambda e: e.tensor_scalar(out=m1[:], in0=dst[:], scalar1=-math.pi, scalar2=TWO_PI, op0=ALU.is_lt, op1=ALU.mult),
                          reads=[dst], writes=[m1])
                    tk.op("dve", lambda e: e.tensor_tensor(out=dst[:], in0=dst[:], in1=m1[:], op=ALU.add), reads=[dst, m1], writes=[dst])
                    tk.op("dve", lambda e: e.tensor_scalar(out=dst[:], in0=dst[:], scalar1=math.pi, scalar2=-math.pi, op0=ALU.min, op1=ALU.max),
                          reads=[dst], writes=[dst])
                wrap(ang, r1)
                tk.op("act", lambda e: e.activation(out=Stab[:], in_=r1[:], func=AF.Sin), reads=[r1], writes=[Stab])
                tk.op("dve", lambda e: e.tensor_scalar(out=ang[:], in0=r1[:], scalar1=0.5 * math.pi, scalar2=None, op0=ALU.add),
                      reads=[r1], writes=[ang])
                wrap(ang, r1)
                tk.op("act", lambda e: e.activation(out=Ctab[:], in_=r1[:], func=AF.Sin), reads=[r1], writes=[Ctab])
                tk.barrier()
            if "rope" in self.dbg:
                o = self.dout("dbg_rope", [128, 2 * T], F32)
                tk.dma("sp", o[:, 0:T], Ctab[:], reads=[Ctab], writes=[o])
                tk.dma("sp", o[:, T:2 * T], Stab[:], reads=[Stab], writes=[o])

            self.Ctab, self.Stab, self.modvec = Ctab, Stab, modvec
            for l in range(DEPTH):
                self.layer(l, top)

            with contextlib.ExitStack() as es:
                hpool = self.pool(es, "fh", [128, 4, 128], F32, 3)
                ypool = self.pool(es, "fy", [128, D], F32, 2)
                HTv = self.HT.ap.rearrange("(c p) t -> p c t", p=128)
                for tt in range(NQ):
                    yt = ypool()
                    for g4 in range(KC // 4):
                        hb = hpool()
                        tk.dma("sp", hb[:], HTv[:, g4 * 4:(g4 + 1) * 4, tt * 128:(tt + 1) * 128],
                               reads=[self.HTc[(tt * 128) // CH]], writes=[hb])
                        ps = self.ps()
                        for j in range(4):
                            tk.op("pe", lambda e, ps=ps, j=j, hb=hb: e.transpose(out=ps[:, j * 128:(j + 1) * 128], in_=hb[:, j, :],
                                                                               identity=ident_f()),
                                  reads=[hb, cst], writes=[ps], waw=(j == 0))
                        tk.op("act", lambda e, yt=yt, ps=ps, g4=g4: e.activation(out=yt[:, g4 * 512:(g4 + 1) * 512], in_=ps[:], func=AF.Copy),
                              reads=[ps], writes=[yt], waw=False)
                    tk.dma("pool", Y[tt * 128:(tt + 1) * 128, :], yt[:], reads=[yt], writes=[Y])
                tk.barrier()
        return nc

    def layer(self, l, top):
        import os
        stop = int(os.environ.get("KSTOP", "99"))
        self.phaseA(l)
        if stop <= 10:
            return
        self.phaseB(l)
        if stop <= 20:
            return
        self.phaseR(l)
        if stop <= 30:
            return
        self.phaseC(l)
        if l == 0:
            self.tap("hT1", self.HTc[0], self.HT.ap, [D, self.T], F32)
        if stop <= 40:
            return
        self.phaseD(l)

    def norm_to_uT(self, es, l, which, uT):
        tk, CH, NCH = self.tk, self.CH, self.NCH
        cst = self.cst
        hpool = self.pool(es, "nh", [128, CH], F32, 4)
        sqpool = self.pool(es, "nsq", [128, CH], F32, 3)
        rstd = self.sb(es, "rstd", [128, CH])
        gofs = 0 if which == 0 else 32
        shofs = 0 if which == 0 else 96
        for tc in range(NCH):
            hb = self.HTc[tc]
            sl = slice(tc * CH, (tc + 1) * CH)
            pss = self.ps()
            for dc in range(KC):
                ht = hpool()
                tk.dma("sp", ht[:], self.HT.ap[dc * 128:(dc + 1) * 128, sl], reads=[hb], writes=[ht])
                sq = sqpool()
                tk.op("act", lambda e, sq=sq, ht=ht: e.activation(out=sq[:], in_=ht[:], func=AF.Square), reads=[ht], writes=[sq])
                tk.op("pe", lambda e, sq=sq, dc=dc, pss=pss: e.matmul(pss[:, 0:CH], lhsT=cst[:, 128:256], rhs=sq[:], start=(dc == 0), stop=(dc == KC - 1)),
                      reads=[sq, cst], writes=[pss], waw=(dc == 0))
            tk.op("act", lambda e, pss=pss: e.activation(out=rstd[:], in_=pss[:, 0:CH], func=AF.Sqrt, scale=1.0 / D, bias=self.epsb[:, 0:1]),
                  reads=[pss, self.epsb], writes=[rstd])
            tk.op("dve", lambda e: e.reciprocal(out=rstd[:], in_=rstd[:]), reads=[rstd], writes=[rstd])
            for dc in range(KC):
                ht = hpool()
                tk.dma("sp", ht[:], self.HT.ap[dc * 128:(dc + 1) * 128, sl], reads=[hb], writes=[ht])
                sq = sqpool()
                tk.op("dve", lambda e, sq=sq, ht=ht, dc=dc: e.scalar_tensor_tensor(out=sq[:], in0=ht[:], scalar=self.gv[:, l, gofs + dc:gofs + dc + 1],
                                                                               in1=rstd[:], op0=ALU.mult, op1=ALU.mult),
                      reads=[ht, self.gv, rstd], writes=[sq])
                tk.op("act", lambda e, sq=sq, dc=dc: e.activation(out=uT[:, dc, sl], in_=sq[:], func=AF.Identity,
                                                                  bias=self.modc[:, l, shofs + dc:shofs + dc + 1]),
                      reads=[sq, self.modc], writes=[uT], waw=False)

    def phaseA(self, l):
        nc, tk = self.nc, self.tk
        S, T, NQ, CH, NCH = self.S, self.T, self.NQ, self.CH, self.NCH
        cst, vecs, Ctab, Stab = self.cst, self.vecs, self.Ctab, self.Stab
        vb = l * NV
        if l == 0:
            self.QT = self.dint("QT", [2048, T], BF16)
            self.FAGI = self.dint("FAGI", [2048, T], BF16)
            self.FAGO = self.dint("FAGO", [NCORES * 2048, T], BF16)
            self.VAGI = self.dint("VAGI", [T, 1024], BF16)
            self.VAGO = self.dint("VAGO", [S, 1024], BF16)
            self.GN = self.dint("GN", [T, 48], F32)
            self.RXT = self.dint("RXT", [2048, T + 3], F32)
            self.RYG = self.dint("RYG", [2048, T], F32)
            self.SGA = self.dint("SGA", [D, T], F32)
            self.SGR = self.dint("SGR", [D, T], F32)
            self.UHI = self.dint("UHI", [128, KC * 3], BF16)
            self.UHO = self.dint("UHO", [NCORES * 128, KC * 3], BF16)
        WIN = self.WB["w_in"][l][1]
        WINv = WIN.ap.rearrange("(kc p) n -> p kc n", p=128)
        with contextlib.ExitStack() as es:
            uT = self.sb(es, "uT", [128, KC, T], BF16)
            with contextlib.ExitStack() as es2:
                self.norm_to_uT(es2, l, 0, uT)
                tk.barrier()
            if "uT" in self.dbg and l == 0:
                o = self.dout("dbg_uT", [128, KC * T], F32)
                tk.dma("pool", o.ap, uT[:].rearrange("p k t -> p (k t)"), reads=[uT], writes=[o])
            import os
            stop = int(os.environ.get("KSTOP", "99"))
            if stop <= 1:
                tk.barrier()
                return
            uh = self.sb(es, "uh", [128, KC, 3], BF16)
            uh2 = self.sb(es, "uh2", [128, KC, 3], BF16)
            tk.dma("sp", self.UHI.ap.rearrange("p (k j) -> p k j", j=3), uT[:, :, T - 3:T], reads=[uT], writes=[self.UHI])
            tk.allgather(self.UHO.ap.opt(), self.UHI.ap.opt(), reads=[self.UHI], writes=[self.UHO])
            pid = self.pid
            prev = (pid + (NCORES - 1)) % NCORES
            tk.dma("sp", uh[:], self.UHO.ap[bass.ds(prev * 128, 128), :].rearrange("p (k j) -> p k j", j=3), reads=[self.UHO], writes=[uh])
            tk.op("dve", lambda e: e.tensor_scalar(out=uh2[:], in0=uh[:], scalar1=cst[:, 513:514], scalar2=None, op0=ALU.mult),
                  reads=[uh, cst], writes=[uh2])

            wpool = self.pool(es, "wp", [128, KC, 512], BF16, 2)
            wk = [self.pool(es, "wk%d" % i, [128, CH], F32, 2) for i in range(3)]
            obp = self.pool(es, "ob", [128, CH], BF16, 3)
            ofp = self.pool(es, "of", [128, CH], F32, 3)
            obv = self.pool(es, "obv", [128, 512], BF16, 2)

            def load_panel(c0, w):
                pn = wpool()
                for q4 in range(4):
                    tk.dma("sp", pn[:, q4 * 8:(q4 + 1) * 8, 0:w], WINv[:, q4 * 8:(q4 + 1) * 8, c0:c0 + w], reads=[WIN], writes=[pn])
                return pn

            def rope_store(src, tsl, dst_buf, dst_ap):
                ps3 = self.ps()
                tk.op("pe", lambda e: e.matmul(ps3[:, 0:CH], lhsT=cst[:, 256:384], rhs=src[:], start=True, stop=True),
                      reads=[src, cst], writes=[ps3])
                t1 = wk[0]()
                t2 = wk[1]()
                tk.op("dve", lambda e: e.tensor_tensor(out=t1[:], in0=src[:], in1=Ctab[:, tsl], op=ALU.mult), reads=[src, Ctab], writes=[t1])
                tk.op("dve", lambda e: e.tensor_tensor(out=t2[:], in0=ps3[:, 0:CH], in1=Stab[:, tsl], op=ALU.mult), reads=[ps3, Stab], writes=[t2])
                ob = obp()
                tk.op("dve", lambda e: e.tensor_tensor(out=ob[:], in0=t1[:], in1=t2[:], op=ALU.add), reads=[t1, t2], writes=[ob])
                tk.dma("pool", dst_ap, ob[:], reads=[ob], writes=[dst_buf])

            def evac(kind, fi, tc, ps, M):
                tsl = slice(tc * CH, (tc + 1) * CH)
                if kind in ("q", "ks", "kw"):
                    gcol = {"q": V_QN, "ks": V_KN + 1, "kw": V_KN + 2}[kind]
                    sq = wk[0]()
                    tk.op("act", lambda e: e.activation(out=sq[:], in_=ps[:, 0:CH], func=AF.Square), reads=[ps], writes=[sq])
                    ps2 = self.ps()
                    tk.op("pe", lambda e: e.matmul(ps2[:, 0:CH], lhsT=cst[:, 128:256], rhs=sq[:], start=True, stop=True),
                          reads=[sq, cst], writes=[ps2])
                    rs = wk[1]()
                    tk.op("act", lambda e: e.activation(out=rs[:], in_=ps2[:, 0:CH], func=AF.Sqrt, scale=1.0 / 128, bias=self.epsb[:, 0:1]),
                          reads=[ps2, self.epsb], writes=[rs])
                    tk.op("dve", lambda e: e.reciprocal(out=rs[:], in_=rs[:]), reads=[rs], writes=[rs])
                    xn = wk[2]()
                    tk.op("dve", lambda e: e.scalar_tensor_tensor(out=xn[:], in0=ps[:, 0:CH], scalar=vecs[:, vb + gcol:vb + gcol + 1], in1=rs[:],
                                                                  op0=ALU.mult, op1=ALU.mult), reads=[ps, vecs, rs], writes=[xn])
                    if kind == "q":
                        rope_store(xn, tsl, self.QT, self.QT.ap[fi * 128:(fi + 1) * 128, tsl])
                    else:
                        base = {"ks": 1024, "kw": 1536}[kind]
                        rope_store(xn, tsl, self.FAGI, self.FAGI.ap[base + fi * 128:base + (fi + 1) * 128, tsl])
                elif kind == "kc":
                    xn = wk[2]()
                    tk.op("act", lambda e: e.activation(out=xn[:], in_=ps[:, 0:CH], func=AF.Copy), reads=[ps], writes=[xn])
                    rope_store(xn, tsl, self.FAGI, self.FAGI.ap[fi * 128:(fi + 1) * 128, tsl])
                elif kind == "vc":
                    ob = obp()
                    tk.op("act", lambda e: e.activation(out=ob[:], in_=ps[:, 0:CH], func=AF.Copy), reads=[ps], writes=[ob])
                    tk.dma("pool", self.FAGI.ap[512 + fi * 128:512 + (fi + 1) * 128, tsl], ob[:], reads=[ob], writes=[self.FAGI])
                else:
                    of = ofp()
                    if kind == "rx":
                        tk.op("act", lambda e: e.activation(out=of[:], in_=ps[:, 0:CH], func=AF.Copy), reads=[ps], writes=[of])
                        tk.dma("pool", self.RXT.ap[fi * 128:(fi + 1) * 128, 3 + tc * CH:3 + (tc + 1) * CH], of[:], reads=[of], writes=[self.RXT])
                    elif kind == "ry":
                        self.gelu(ps, of, wk)
                        tk.dma("pool", self.RYG.ap[fi * 128:(fi + 1) * 128, tsl], of[:], reads=[of], writes=[self.RYG])
                    else:
                        dst = self.SGA if kind == "ga" else self.SGR
                        tk.op("act", lambda e: e.activation(out=of[:], in_=ps[:, 0:CH], func=AF.Sigmoid), reads=[ps], writes=[of])
                        tk.dma("pool", dst.ap[fi * 128:(fi + 1) * 128, tsl], of[:], reads=[of], writes=[dst])

            def seg_fm(kind, col0, nchunks):
                for p0 in range(0, nchunks, 4):
                    npc = min(4, nchunks - p0)
                    pn = load_panel(col0 + p0 * 128, npc * 128)
                    for j in range(npc):
                        fi = p0 + j
                        for tc in range(NCH):
                            ps = self.ps()
                            for kc in range(KC):
                                tk.op("pe", lambda e, kc=kc, j=j, tc=tc, ps=ps, pn=pn: e.matmul(
                                    ps[:, 0:CH], lhsT=pn[:, kc, j * 128:(j + 1) * 128], rhs=uT[:, kc, tc * CH:(tc + 1) * CH],
                                    start=(kc == 0), stop=(kc == KC - 1)), reads=[pn, uT], writes=[ps], waw=(kc == 0))
                            evac(kind, fi, tc, ps, 128)
                        if kind == "rx":
                            ps = self.ps()
                            for kc in range(KC):
                                tk.op("pe", lambda e, kc=kc, j=j, ps=ps, pn=pn: e.matmul(
                                    ps[:, 0:3], lhsT=pn[:, kc, j * 128:(j + 1) * 128], rhs=uh2[:, kc, :],
                                    start=(kc == 0), stop=(kc == KC - 1)), reads=[pn, uh2], writes=[ps], waw=(kc == 0))
                            of = ofp()
                            tk.op("act", lambda e, ps=ps, of=of: e.activation(out=of[:, 0:3], in_=ps[:, 0:3], func=AF.Copy), reads=[ps], writes=[of])
                            tk.dma("pool", self.RXT.ap[fi * 128:(fi + 1) * 128, 0:3], of[:, 0:3], reads=[of], writes=[self.RXT])

            def seg_tm(col0, w, fn):
                pn = load_panel(col0, w)
                for tt in range(NQ):
                    ps = self.ps()
                    for kc in range(KC):
                        tk.op("pe", lambda e, kc=kc, tt=tt, ps=ps, pn=pn: e.matmul(
                            ps[:, 0:w], lhsT=uT[:, kc, tt * 128:(tt + 1) * 128], rhs=pn[:, kc, 0:w],
                            start=(kc == 0), stop=(kc == KC - 1)), reads=[pn, uT], writes=[ps], waw=(kc == 0))
                    fn(tt, ps)

            def ev_v(off):
                def f(tt, ps):
                    ob = obv()
                    tk.op("act", lambda e: e.activation(out=ob[:, 0:512], in_=ps[:, 0:512], func=AF.Copy), reads=[ps], writes=[ob])
                    tk.dma("pool", self.VAGI.ap[tt * 128:(tt + 1) * 128, off:off + 512], ob[:, 0:512], reads=[ob], writes=[self.VAGI])
                return f

            def ev_gn(tt, ps):
                of = ofp()
                tk.op("act", lambda e: e.activation(out=of[:, 0:48], in_=ps[:, 0:48], func=AF.Sigmoid), reads=[ps], writes=[of])
                tk.dma("pool", self.GN.ap[tt * 128:(tt + 1) * 128, :], of[:, 0:48], reads=[of], writes=[self.GN])

            if stop <= 2:
                tk.barrier()
                return
            seg_fm("kc", SEG_KC, 4)
            if stop <= 3:
                tk.barrier()
                return
            seg_fm("vc", SEG_VC, 4)
            seg_fm("ks", SEG_KS, 4)
            seg_fm("kw", SEG_KW, 4)
            tk.allgather(self.FAGO.ap.opt(), self.FAGI.ap.opt(), reads=[self.FAGI], writes=[self.FAGO])
            seg_tm(SEG_VS, 512, ev_v(0))
            seg_tm(SEG_VW, 512, ev_v(512))
            tk.allgather(self.VAGO.ap.opt(), self.VAGI.ap.opt(), reads=[self.VAGI], writes=[self.VAGO])
            seg_tm(SEG_GN, 48, ev_gn)
            seg_fm("q", SEG_Q, 16)
            seg_fm("rx", SEG_RX, 16)
            seg_fm("ry", SEG_RY, 16)
            seg_fm("ga", SEG_GA, 32)
            seg_fm("gr", SEG_GR, 32)
            tk.barrier()
        for nm, bufp, shp, dt in (("QT", self.QT, [2048, T], BF16), ("FAGI", self.FAGI, [2048, T], BF16), ("VAGI", self.VAGI, [T, 1024], BF16),
                                  ("GN", self.GN, [T, 48], F32), ("RXT", self.RXT, [2048, T + 3], F32), ("RYG", self.RYG, [2048, T], F32),
                                  ("SGA", self.SGA, [D, T], F32), ("VAGO", self.VAGO, [S, 1024], BF16), ("FAGO", self.FAGO, [NCORES * 2048, T], BF16)):
            if l == 0:
                self.tap(nm, bufp, bufp.ap, shp, dt)


    def phaseB(self, l):
        nc, tk = self.nc, self.tk
        S, T, NQ, CH, NCH, NKT, NCMP, NCT = self.S, self.T, self.NQ, self.CH, self.NCH, self.NKT, self.NCMP, self.NCT
        cst, vecs = self.cst, self.vecs
        PAD = S
        SCALE = 128.0 ** -0.5
        if l == 0:
            self.KCT = self.dint("KCT", [512, S], BF16)
            self.VCT = self.dint("VCT", [512, S], BF16)
            self.KSW = self.dint("KSW", [1024, PAD + S], BF16)
            self.VP = self.dint("VP", [PAD + S, 1024], BF16)
            self.ATT = self.dint("ATT", [2048, T], BF16)
            self.TB_CMPB = self.din("tb_cmpb", [NQ * 128, NCT * 512])
            self.TB_OV = self.din("tb_ov", [NQ * 128, NCT * 128])
            self.TB_SEL = self.din("tb_sel", [NQ * 128, 3 * 128])
            self.TB_WB = self.din("tb_wb", [1, NQ * 5 * 512])
            self.TB_E = self.din("tb_e", [128, NKT * 128])
            self.TB_CAUS = self.din("tb_caus", [128, 2 * 512])
            self.CMPW = [self.din("cmp_w_k", [self.DEPTH * 32 * 128, 128]), self.din("cmp_w_v", [self.DEPTH * 32 * 128, 128])]
            self.CMPPE = self.din("cmp_peT", [128, self.DEPTH * 64])
            self.KN0 = self.din("k_norm0", [self.DEPTH, 128])
        pid = self.pid
        tilebase = (pid % NCORES) * NQ
        with contextlib.ExitStack() as es:
            zt = self.sb(es, "zt", [128, 4096], BF16)
            tk.op("pool", lambda e: e.memset(zt[:], 0.0), writes=[zt])
            if l == 0:
                for dst in (self.KSW,):
                    for r4 in range(8):
                        for c0 in range(0, PAD, 4096):
                            w = min(4096, PAD - c0)
                            tk.dma("sp", dst.ap[r4 * 128:(r4 + 1) * 128, c0:c0 + w], zt[:, 0:w], reads=[zt], writes=[dst])
                for r0 in range(0, PAD, 512):
                    tk.dma("sp", self.VP.ap[r0:r0 + 512, :].rearrange("(a p) n -> p a n", p=128),
                           zt[:, 0:4096].rearrange("p (a n) -> p a n", a=4), reads=[zt], writes=[self.VP])
            E1 = (NCORES - (pid % NCORES)) * T
            for r in range(NCORES):
                base = r * 2048
                tk.dma("sp", self.KCT.ap[:, r * T:(r + 1) * T], self.FAGO.ap[base:base + 512, :], reads=[self.FAGO], writes=[self.KCT])
                tk.dma("sp", self.VCT.ap[:, r * T:(r + 1) * T], self.FAGO.ap[base + 512:base + 1024, :], reads=[self.FAGO], writes=[self.VCT])
                tk.dma("sp", self.KSW.ap[:, r * T:r * T + S + T][:, bass.ds(E1, T)],
                       self.FAGO.ap[base + 1024:base + 2048, :], reads=[self.FAGO], writes=[self.KSW], waw=True)
            pidp = self.pidp
            E1p = (NCORES - (pidp % NCORES)) * T
            VPf = self.VP.ap.rearrange("r n -> (r n)")
            for r2 in range(0, NCORES, 2):
                tk.dma("pool", VPf[r2 * T * 1024:(r2 * T + S + 2 * T) * 1024][bass.ds(E1p * 1024, 2 * T * 1024)],
                       self.VAGO.ap[r2 * T:(r2 + 2) * T, :].rearrange("r n -> (r n)"), reads=[self.VAGO], writes=[self.VP], waw=True)

            QTs = self.sb(es, "QTs", [128, 16, T], BF16)
            tk.dma("sp", QTs[:], self.QT.ap.rearrange("(h p) t -> p h t", p=128), reads=[self.QT], writes=[QTs])
            Etab = self.sb(es, "Etab", [128, NKT * 128], BF16)
            tk.dma("pool", Etab[:], self.TB_E.ap, reads=[self.TB_E], writes=[Etab])
            caus = self.sb(es, "caus", [128, 2 * 512], BF16)
            tk.dma("pool", caus[:], self.TB_CAUS.ap, reads=[self.TB_CAUS], writes=[caus])
            wbrow = self.sb(es, "wbrow", [1, NQ * 5 * 512], BF16)
            tk.dma("pool", wbrow[:], self.TB_WB.ap, reads=[self.TB_WB], writes=[wbrow])
            onesb = self.sb(es, "onesb", [128, 128], BF16)
            tk.op("pool", lambda e: e.memset(onesb[:], 1.0), writes=[onesb])
            gn = self.sb(es, "gn", [128, NQ, 48])
            tk.dma("sp", gn[:], self.GN.ap.rearrange("(i p) c -> p i c", p=128), reads=[self.GN], writes=[gn])
            kcmpT = self.sb(es, "kcmpT", [128, 4, NCT * 128], BF16)
            vcmp = self.sb(es, "vcmp", [128, NCT, 4, 128], BF16)
            tk.op("pool", lambda e: e.memset(kcmpT[:], 0.0), writes=[kcmpT])
            tk.op("pool", lambda e: e.memset(vcmp[:], 0.0), writes=[vcmp])

            with contextlib.ExitStack() as es2:
                wkv = [self.sb(es2, "cw%d" % i, [128, 32, 128], BF16) for i in range(2)]
                peT = self.sb(es2, "peT", [128, 64], BF16)
                kn0 = self.sb(es2, "kn0", [128, 128])
                for i in range(2):
                    tk.dma("pool", wkv[i][:], self.CMPW[i].ap[l * 4096:(l + 1) * 4096, :].rearrange("(a p) e -> p a e", p=128),
                           reads=[self.CMPW[i]], writes=[wkv[i]])
                tk.dma("pool", peT[:], self.CMPPE.ap[:, l * 64:(l + 1) * 64], reads=[self.CMPPE], writes=[peT])
                tk.dma("sp", kn0[:], self.KN0.ap[l:l + 1, :].partition_broadcast(128), reads=[self.KN0], writes=[kn0])
                brow = self.sb(es2, "brow", [1, 256], BF16)
                bias = self.sb(es2, "cbias", [128, 256])
                psb = self.ps()
                for i in range(2):
                    for a in range(32):
                        tk.op("pe", lambda e, i=i, a=a: e.matmul(psb[0:1, i * 128:(i + 1) * 128], lhsT=peT[:, i * 32 + a:i * 32 + a + 1], rhs=wkv[i][:, a, :],
                                                                 start=(a == 0), stop=(a == 31)), reads=[peT, wkv[i]], writes=[psb], waw=(i == 0 and a == 0))
                tk.op("act", lambda e: e.activation(out=brow[:], in_=psb[0:1, 0:256], func=AF.Copy), reads=[psb], writes=[brow])
                psb2 = self.ps()
                tk.op("pe", lambda e: e.matmul(psb2[:, 0:256], lhsT=onesb[0:1, :], rhs=brow[:], start=True, stop=True), reads=[onesb, brow], writes=[psb2])
                tk.op("act", lambda e: e.activation(out=bias[:], in_=psb2[:, 0:256], func=AF.Copy), reads=[psb2], writes=[bias])
                srcp = self.pool(es2, "csrc", [128, S], BF16, 2)
                xw = self.pool(es2, "cxw", [128, 128], F32, 2)
                xj = self.sb(es2, "cxj", [128, 128])
                st1 = self.pool(es2, "cst1", [128, 1], F32, 2)
                knb = self.pool(es2, "cknb", [128, 128], BF16, 2)
                for g in range(4):
                    for i, SRC in enumerate((self.KCT, self.VCT)):
                        src = srcp()
                        tk.dma("sp", src[:], SRC.ap[g * 128:(g + 1) * 128, :], reads=[SRC], writes=[src])
                        for nt in range(NCT):
                            nn = min(128, NCMP - nt * 128)
                            ps = self.ps()
                            for a in range(32):
                                c0 = nt * 2048 + a
                                tk.op("pe", lambda e, a=a, c0=c0, nn=nn, ps=ps, src=src, i=i: e.matmul(
                                    ps[0:nn, 0:128], lhsT=src[:, c0:c0 + 16 * (nn - 1) + 1:16], rhs=wkv[i][:, a, :],
                                    start=(a == 0), stop=(a == 31)), reads=[src, wkv[i]], writes=[ps], waw=(a == 0))
                            x = xw()
                            tk.op("dve", lambda e, x=x, ps=ps, nn=nn, i=i: e.tensor_tensor(out=x[0:nn, :], in0=ps[0:nn, 0:128], in1=bias[0:nn, i * 128:(i + 1) * 128],
                                                                                         op=ALU.add), reads=[ps, bias], writes=[x])
                            if i == 1:
                                tk.op("act", lambda e, x=x, nn=nn, nt=nt, g=g: e.activation(out=vcmp[0:nn, nt, g, :], in_=x[0:nn, :], func=AF.Copy),
                                      reads=[x], writes=[vcmp], waw=False)
                                continue
                            ss = st1()
                            tk.op("dve", lambda e, x=x, nn=nn, ss=ss: e.scalar_tensor_tensor(out=xj[0:nn, :], in0=x[0:nn, :], scalar=1.0, in1=x[0:nn, :],
                                                                                           op0=ALU.mult, op1=ALU.mult, accum_out=ss[0:nn, :]),
                                  reads=[x], writes=[xj, ss])
                            tk.op("act", lambda e, nn=nn, ss=ss: e.activation(out=ss[0:nn, :], in_=ss[0:nn, :], func=AF.Sqrt, scale=1.0 / 128, bias=self.epsb[0:nn, 0:1]),
                                  reads=[ss, self.epsb], writes=[ss])
                            tk.op("dve", lambda e, nn=nn, ss=ss: e.reciprocal(out=ss[0:nn, :], in_=ss[0:nn, :]), reads=[ss], writes=[ss])
                            kb = knb()
                            tk.op("dve", lambda e, x=x, nn=nn, ss=ss, kb=kb: e.scalar_tensor_tensor(out=kb[0:nn, :], in0=x[0:nn, :], scalar=ss[0:nn, 0:1], in1=kn0[0:nn, :],
                                                                                                 op0=ALU.mult, op1=ALU.mult), reads=[x, ss, kn0], writes=[kb])
                            tk.op("pe", lambda e, nn=nn, kb=kb: e.transpose(out=self.psT[:, 0:nn], in_=kb[0:nn, :], identity=self.identb[0:nn, 0:nn]),
                                  reads=[kb, self.identb], writes=[self.psT])
                            tk.op("act", lambda e, nn=nn, nt=nt, g=g: e.activation(out=kcmpT[:, g, nt * 128:nt * 128 + nn], in_=self.psT[:, 0:nn], func=AF.Copy),
                                  reads=[self.psT], writes=[kcmpT], waw=False)
                tk.barrier()
            if "kcmp" in self.dbg and l == 0:
                o = self.dout("dbg_kcmpT", [128, 4 * NCT * 128], F32)
                tk.dma("pool", o.ap, kcmpT[:].rearrange("p g n -> p (g n)"), reads=[kcmpT], writes=[o])
                o2 = self.dout("dbg_vcmp", [128, NCT * 4 * 128], F32)
                tk.dma("pool", o2.ap, vcmp[:].rearrange("p a g d -> p (a g d)"), reads=[vcmp], writes=[o2])

            self.ps_rot = self.psum[4:7]
            cmpb_p = self.pool(es, "cmpb", [128, NCT, 512], BF16, 2)
            ov_p = self.pool(es, "ovr", [128, NCT, 128], BF16, 2)
            sel_p = self.pool(es, "selt", [128, 3, 128], F32, 2)
            pT_p = self.pool(es, "pT", [128, 512], BF16, 4)
            ksl_p = self.pool(es, "ksl", [128, 512], BF16, 3)
            vsl_p = [self.sb(es, "vsl%d" % i, [128, 4, 132], BF16) for i in range(3)]
            for v in vsl_p:
                tk.op("pool", lambda e, v=v: e.memset(v[:], 1.0), writes=[v])
            vsl_i = [0]
            attn = self.sb(es, "attn", [128, 4, 128])
            attnb = self.sb(es, "attnb", [128, 4, 128], BF16)
            att_o = self.pool(es, "atto", [128, 4, 128], BF16, 2)
            sm = self.pool(es, "sm", [128, 16], F32, 3)
            imp = self.sb(es, "imp", [128, 128])
            sc2 = self.sb(es, "sc2", [128, 128])
            m8 = self.sb(es, "m8", [128, 16])
            nb = self.sb(es, "nb", [128, 128], BF16)
            nbT4 = self.sb(es, "nbT4", [128, 4, 128], BF16)
            if "sel" in self.dbg and l == 0:
                dsel = self.dout("dbg_sel", [NQ * 4 * 128, 128], F32)
            self.dbgt = self.sb(es, "dbgt", [128, 512])
            if "br" in self.dbg and l == 0:
                dbr = [self.dout("dbg_br%d" % k, [NQ * 4 * 128, 512], F32) for k in range(3)]

            def brtap(k, i, g):
                if "br" in self.dbg and l == 0:
                    tk.dma("pool", dbr[k].ap[(i * 4 + g) * 128:(i * 4 + g + 1) * 128, :], attn[:].rearrange("p z d -> p (z d)"), reads=[attn], writes=[dbr[k]])

            def sweep(i, g, KT, voff, nr, sel_bias, acc_banks):
                qblk = QTs[:, 4 * g:4 * g + 4, i * 128:(i + 1) * 128]
                nslab = (nr + 3) // 4
                VPv = self.VP.ap.rearrange("(a p) n -> p a n", p=128)
                span = (S - T) // 128 + 4
                for m in range(nslab):
                    ks = ksl_p()
                    vs = vsl_p[vsl_i[0] % 3]
                    vsl_i[0] += 1
                    a0 = PAD // 128 + i - 4 * m - 3
                    tk.dma("sp", ks[:], self.KSW.ap[KT + g * 128:KT + (g + 1) * 128, a0 * 128:a0 * 128 + 512], reads=[self.KSW], writes=[ks])
                    tk.dma("sp", vs[:, :, 0:128], VPv[:, a0:a0 + 4, voff + g * 128:voff + (g + 1) * 128], reads=[self.VP], writes=[vs])
                    for t4 in (3, 2, 1, 0):
                        r = 4 * m + (3 - t4)
                        if r >= nr:
                            continue
                        ps = self.ps()
                        extra = sel_bias(r)
                        tk.op("pe", lambda e, ps=ps, ks=ks, t4=t4: e.matmul(ps[:, 0:512], lhsT=ks[:, t4 * 128:(t4 + 1) * 128], rhs=qblk, start=True, stop=(len(extra) == 0)),
                              reads=[ks, QTs], writes=[ps])
                        for xi, (lh, rh, rd) in enumerate(extra):
                            tk.op("pe", lambda e, ps=ps, lh=lh, rh=rh, xi=xi: e.matmul(ps[:, 0:512], lhsT=lh, rhs=rh, start=False, stop=(xi == len(extra) - 1)),
                                  reads=rd, writes=[ps], waw=False)
                        pT = pT_p()
                        if "sw" in self.dbg and l == 0 and i == 0 and g == 0 and r == 0 and KT == 512:
                            dS = self.dout("dbg_swS", [128, 512], F32)
                            tk.op("dve", lambda e, ps=ps: e.tensor_copy(out=self.dbgt[:], in_=ps[:, 0:512]), reads=[ps], writes=[self.dbgt])
                            tk.dma("pool", dS.ap, self.dbgt[:], reads=[self.dbgt], writes=[dS])
                            dK = self.dout("dbg_swK", [128, 512], F32)
                            tk.dma("pool", dK.ap, ks[:], reads=[ks], writes=[dK])
                            dV = self.dout("dbg_swV", [128, 4 * 132], F32)
                            tk.dma("pool", dV.ap, vs[:].rearrange("p a d -> p (a d)"), reads=[vs], writes=[dV])
                        tk.op("act", lambda e, pT=pT, ps=ps: e.activation(out=pT[:], in_=ps[:, 0:512], func=AF.Exp, scale=SCALE), reads=[ps], writes=[pT])
                        if "sw" in self.dbg and l == 0 and i == 0 and g == 0 and r == 0 and KT == 512:
                            dP = self.dout("dbg_swP", [128, 512], F32)
                            tk.dma("pool", dP.ap, pT[:], reads=[pT], writes=[dP])
                        for z in range(4):
                            ab = acc_banks[z // 2]
                            tk.op("pe", lambda e, ab=ab, z=z, pT=pT, vs=vs, t4=t4, r=r: e.matmul(
                                ab[:, (z % 2) * 256:(z % 2) * 256 + 129], lhsT=pT[:, z * 128:(z + 1) * 128], rhs=vs[:, t4, 0:129],
                                start=(r == 0 and z % 2 == 0), stop=(r == nr - 1), skip_group_check=True), reads=[pT, vs], writes=[ab], waw=(r == 0 and z % 2 == 0))

            def finish(acc_banks, i, g, br, first):
                s4 = sm()
                for z in range(4):
                    ab = acc_banks[z // 2]
                    c = (z % 2) * 256
                    tk.op("dve", lambda e, ab=ab, c=c, z=z: e.tensor_scalar(out=s4[:, z:z + 1], in0=ab[:, c + 128:c + 129], scalar1=1e-30, scalar2=None, op0=ALU.max),
                          reads=[ab], writes=[s4], waw=False)
                tk.op("dve", lambda e: e.reciprocal(out=s4[:, 0:4], in_=s4[:, 0:4]), reads=[s4], writes=[s4])
                for z in range(4):
                    h = 4 * g + z
                    tk.op("dve", lambda e, z=z, h=h: e.tensor_tensor(out=s4[:, 4 + z:5 + z], in0=s4[:, z:z + 1], in1=gn[:, i, 3 * h + br:3 * h + br + 1], op=ALU.mult),
                          reads=[s4, gn], writes=[s4])
                for z in range(4):
                    ab = acc_banks[z // 2]
                    c = (z % 2) * 256
                    if first:
                        tk.op("dve", lambda e, ab=ab, c=c, z=z: e.tensor_scalar(out=attn[:, z, :], in0=ab[:, c:c + 128], scalar1=s4[:, 4 + z:5 + z], scalar2=None, op0=ALU.mult),
                              reads=[ab, s4], writes=[attn], waw=False)
                    else:
                        tk.op("dve", lambda e, ab=ab, c=c, z=z: e.scalar_tensor_tensor(out=attn[:, z, :], in0=ab[:, c:c + 128], scalar=s4[:, 4 + z:5 + z], in1=attn[:, z, :],
                                                                                      op0=ALU.mult, op1=ALU.add), reads=[ab, s4, attn], writes=[attn])

            for i in range(NQ):
                cmpb = cmpb_p()
                tk.dma("pool", cmpb[:], self.TB_CMPB.ap[i * 128:(i + 1) * 128, :].rearrange("p (a n) -> p a n", a=NCT), reads=[self.TB_CMPB], writes=[cmpb])
                ovr = ov_p()
                tk.dma("pool", ovr[:], self.TB_OV.ap[i * 128:(i + 1) * 128, :].rearrange("p (a n) -> p a n", a=NCT), reads=[self.TB_OV], writes=[ovr])
                selt = sel_p()
                tk.dma("pool", selt[:], self.TB_SEL.ap[i * 128:(i + 1) * 128, :].rearrange("p (a n) -> p a n", a=3), reads=[self.TB_SEL], writes=[selt])
                for g in range(4):
                    qblk = QTs[:, 4 * g:4 * g + 4, i * 128:(i + 1) * 128]
                    pTs = []
                    for nt in range(NCT):
                        ps = self.ps()
                        tk.op("pe", lambda e, ps=ps, nt=nt: e.matmul(ps[:, 0:512], lhsT=kcmpT[:, g, nt * 128:(nt + 1) * 128], rhs=qblk, start=True, stop=False),
                              reads=[kcmpT, QTs], writes=[ps])
                        tk.op("pe", lambda e, ps=ps, nt=nt: e.matmul(ps[:, 0:512], lhsT=self.identb[:], rhs=cmpb[:, nt, :], start=False, stop=True),
                              reads=[self.identb, cmpb], writes=[ps], waw=False)
                        pT = pT_p()
                        tk.op("act", lambda e, pT=pT, ps=ps: e.activation(out=pT[:], in_=ps[:, 0:512], func=AF.Exp, scale=SCALE), reads=[ps], writes=[pT])
                        pTs.append(pT)
                    accc = [self.psum[0], self.psum[1]]
                    for z in range(4):
                        ab = accc[z // 2]
                        c = (z % 2) * 256
                        for nt in range(NCT):
                            tk.op("pe", lambda e, ab=ab, c=c, z=z, nt=nt: e.matmul(ab[:, c:c + 128], lhsT=pTs[nt][:, z * 128:(z + 1) * 128], rhs=vcmp[:, nt, g, :],
                                                                                   start=(nt == 0), stop=(nt == NCT - 1)), reads=[pTs[nt], vcmp], writes=[ab], waw=(z % 2 == 0 and nt == 0))
                        for nt in range(NCT):
                            tk.op("pe", lambda e, ab=ab, c=c, z=z, nt=nt: e.matmul(ab[:, c + 128:c + 256], lhsT=pTs[nt][:, z * 128:(z + 1) * 128], rhs=ovr[:, nt, :],
                                                                                   start=(nt == 0), stop=(nt == NCT - 1)), reads=[pTs[nt], ovr], writes=[ab], waw=False)
                    s4 = sm()
                    for z in range(4):
                        ab = accc[z // 2]
                        c = (z % 2) * 256
                        tk.op("dve", lambda e, ab=ab, c=c, z=z: e.tensor_reduce(out=s4[:, z:z + 1], in_=ab[:, c + 128:c + 256], axis=AX.X, op=ALU.add),
                              reads=[ab], writes=[s4], waw=False)
                    tk.op("dve", lambda e: e.tensor_scalar(out=s4[:, 0:4], in0=s4[:, 0:4], scalar1=1e-30, scalar2=None, op0=ALU.max), reads=[s4], writes=[s4])
                    tk.op("dve", lambda e: e.reciprocal(out=s4[:, 0:4], in_=s4[:, 0:4]), reads=[s4], writes=[s4])
                    for z in range(4):
                        h = 4 * g + z
                        tk.op("dve", lambda e, z=z, h=h: e.tensor_tensor(out=s4[:, 4 + z:5 + z], in0=s4[:, z:z + 1], in1=gn[:, i, 3 * h:3 * h + 1], op=ALU.mult),
                              reads=[s4, gn], writes=[s4])
                    for z in range(4):
                        ab = accc[z // 2]
                        c = (z % 2) * 256
                        tk.op("dve", lambda e, ab=ab, c=c, z=z: e.tensor_scalar(out=attn[:, z, :], in0=ab[:, c:c + 128], scalar1=s4[:, 4 + z:5 + z], scalar2=None, op0=ALU.mult),
                              reads=[ab, s4], writes=[attn], waw=False)
                        if z == 0:
                            tk.op("dve", lambda e, ab=ab, c=c: e.tensor_scalar(out=imp[:], in0=ab[:, c + 128:c + 256], scalar1=s4[:, 0:1], scalar2=None, op0=ALU.mult),
                                  reads=[ab, s4], writes=[imp])
                        else:
                            tk.op("dve", lambda e, ab=ab, c=c, z=z: e.scalar_tensor_tensor(out=imp[:], in0=ab[:, c + 128:c + 256], scalar=s4[:, z:z + 1], in1=imp[:],
                                                                                          op0=ALU.mult, op1=ALU.add), reads=[ab, s4, imp], writes=[imp])
                    brtap(0, i, g)
                    tk.op("dve", lambda e: e.tensor_tensor(out=imp[:], in0=imp[:], in1=selt[:, 0, :], op=ALU.mult), reads=[imp, selt], writes=[imp])
                    tk.op("dve", lambda e: e.tensor_tensor(out=imp[:], in0=imp[:], in1=selt[:, 1, :], op=ALU.add), reads=[imp, selt], writes=[imp])
                    tk.op("dve", lambda e: e.max(out=m8[:, 0:8], in_=imp[:]), reads=[imp], writes=[m8])
                    tk.op("dve", lambda e: e.match_replace(out=sc2[:], in_to_replace=m8[:, 0:8], in_values=imp[:], imm_value=-1e9), reads=[m8, imp], writes=[sc2])
                    tk.op("dve", lambda e: e.max(out=m8[:, 8:16], in_=sc2[:]), reads=[sc2], writes=[m8])
                    tk.op("dve", lambda e: e.tensor_scalar(out=sc2[:], in0=imp[:], scalar1=m8[:, 15:16], scalar2=None, op0=ALU.is_ge), reads=[imp, m8], writes=[sc2])
                    tk.op("dve", lambda e: e.tensor_tensor(out=sc2[:], in0=sc2[:], in1=selt[:, 2, :], op=ALU.mult), reads=[sc2, selt], writes=[sc2])
                    if "sel" in self.dbg and l == 0:
                        tk.dma("pool", dsel.ap[(i * 4 + g) * 128:(i * 4 + g + 1) * 128, :], sc2[:], reads=[sc2], writes=[dsel])
                    tk.op("dve", lambda e: e.tensor_scalar(out=nb[:], in0=sc2[:], scalar1=-1.0, scalar2=BIG, op0=ALU.add, op1=ALU.mult), reads=[sc2], writes=[nb])
                    tk.op("pe", lambda e: e.transpose(out=self.psT[:, 0:128], in_=nb[:], identity=self.identb[:]), reads=[nb, self.identb], writes=[self.psT])
                    for z in range(4):
                        tk.op("act", lambda e, z=z: e.activation(out=nbT4[:, z, :], in_=self.psT[:, 0:128], func=AF.Copy), reads=[self.psT], writes=[nbT4], waw=(z == 0))
                    accs = [self.psum[2], self.psum[3]]

                    def selb(r):
                        ex = [(Etab[:, r * 128:(r + 1) * 128], nbT4[:], [Etab, nbT4])]
                        if r == 0:
                            ex.append((self.identb[:], caus[:, 0:512], [self.identb, caus]))
                        return ex
                    sweep(i, g, 0, 0, NKT, selb, accs)
                    finish(accs, i, g, 1, False)
                    brtap(1, i, g)
                    accw = [self.psum[0], self.psum[1]]

                    def winb(r):
                        ex = [(onesb[0:1, :], wbrow[0:1, (i * 5 + r) * 512:(i * 5 + r + 1) * 512], [onesb, wbrow])]
                        if r == 0:
                            ex.append((self.identb[:], caus[:, 0:512], [self.identb, caus]))
                        if r == 4:
                            ex.append((self.identb[:], caus[:, 512:1024], [self.identb, caus]))
                        return ex
                    sweep(i, g, 512, 512, 5, winb, accw)
                    finish(accw, i, g, 2, False)
                    brtap(2, i, g)
                    tk.op("act", lambda e: e.activation(out=attnb[:], in_=attn[:], func=AF.Copy), reads=[attn], writes=[attnb])
                    ao = att_o()
                    for z in range(4):
                        tk.op("pe", lambda e, z=z: e.transpose(out=self.psT[:, z * 128:(z + 1) * 128], in_=attnb[:, z, :], identity=self.identb[:]),
                              reads=[attnb, self.identb], writes=[self.psT], waw=(z == 0))
                    tk.op("act", lambda e, ao=ao: e.activation(out=ao[:], in_=self.psT[:, 0:512].rearrange("p (z q) -> p z q", z=4), func=AF.Copy),
                          reads=[self.psT], writes=[ao])
                    tk.dma("pool", self.ATT.ap[4 * g * 128:(4 * g + 4) * 128, i * 128:(i + 1) * 128].rearrange("(z p) q -> p z q", p=128), ao[:],
                           reads=[ao], writes=[self.ATT])
                    if os.environ.get("KBAR", "1") == "1":
                        tk.barrier()
            tk.barrier()
            self.ps_rot = None
        if l == 0:
            self.tap("ATT", self.ATT, self.ATT.ap, [2048, T], BF16)

    def phaseR(self, l):
        nc, tk = self.nc, self.tk
        S, T, NQ, CH, NCH = self.S, self.T, self.NQ, self.CH, self.NCH
        cst, vecs = self.cst, self.vecs
        vb = l * NV
        if l == 0:
            self.HLOC = self.dint("HLOC", [2048, T], F32)
            self.CPT = self.dint("CPT", [2048, T], F32)
            self.RNNT = self.dint("RNNT", [2048, T], BF16)
            self.CARI = self.dint("CARI", [128, 32], F32)
            self.CARO = self.dint("CARO", [NCORES * 128, 32], F32)
            self.RGW = [self.din("rg_w_a", [self.DEPTH * 2048, 128]), self.din("rg_w_x", [self.DEPTH * 2048, 128])]
            self.FLG = self.din("flags", [128, NCORES])
        with contextlib.ExitStack() as es:
            wax = [self.sb(es, "rgw%d" % i, [128, 16, 128], BF16) for i in range(2)]
            for i in range(2):
                tk.dma("pool", wax[i][:], self.RGW[i].ap[l * 2048:(l + 1) * 2048, :].rearrange("(h p) j -> p h j", p=128), reads=[self.RGW[i]], writes=[wax[i]])
            flg = self.sb(es, "flg", [128, NCORES])
            tk.dma("sp", flg[:], self.FLG.ap, reads=[self.FLG], writes=[flg])
            spc = self.sb(es, "spc", [128, 16])
            tk.op("act", lambda e: e.activation(out=spc[:], in_=vecs[:, vb + V_RLAM:vb + V_RLAM + 16], func=AF.Exp, scale=-1.0), reads=[vecs], writes=[spc])
            tk.op("act", lambda e: e.activation(out=spc[:], in_=spc[:], func=AF.Ln, bias=self.oneb[:, 0:1]), reads=[spc, self.oneb], writes=[spc])
            tk.op("dve", lambda e: e.tensor_scalar(out=spc[:], in0=spc[:], scalar1=-8.0, scalar2=None, op0=ALU.mult), reads=[spc], writes=[spc])
            car = self.sb(es, "car", [128, 32])
            zer = self.sb(es, "zer", [128, T])
            tk.op("pool", lambda e: e.memset(zer[:], 0.0), writes=[zer])
            xpp = self.pool(es, "rxp", [128, T + 3], F32, 2)
            P = {n: self.pool(es, "r" + n, [128, T], F32, 2) for n in ("xr", "r", "ig", "a", "x2", "w", "t", "h", "cp")}
            xbp = self.pool(es, "rxb", [128, T], BF16, 2)
            for cc in range(16):
                xp = xpp()
                tk.dma("sp", xp[:], self.RXT.ap[cc * 128:(cc + 1) * 128, :], reads=[self.RXT], writes=[xp])
                xr = P["xr"]()
                cw = lambda j: vecs[:, vb + V_RCW + j * 16 + cc:vb + V_RCW + j * 16 + cc + 1]
                tk.op("dve", lambda e: e.tensor_scalar(out=xr[:], in0=xp[:, 0:T], scalar1=cw(0), scalar2=vecs[:, vb + V_RCB + cc:vb + V_RCB + cc + 1],
                                                       op0=ALU.mult, op1=ALU.add), reads=[xp, vecs], writes=[xr])
                for j in (1, 2, 3):
                    tk.op("dve", lambda e, j=j: e.scalar_tensor_tensor(out=xr[:], in0=xp[:, j:j + T], scalar=cw(j), in1=xr[:], op0=ALU.mult, op1=ALU.add),
                          reads=[xp, vecs, xr], writes=[xr])
                xb = xbp()
                tk.op("act", lambda e: e.activation(out=xb[:], in_=xr[:], func=AF.Copy), reads=[xr], writes=[xb])
                r_, ig = P["r"](), P["ig"]()
                for gi, (dst, bcol) in enumerate(((r_, V_RBA), (ig, V_RBX))):
                    for tc in range(NCH):
                        ps = self.ps()
                        tk.op("pe", lambda e, ps=ps, gi=gi, tc=tc: e.matmul(ps[:, 0:CH], lhsT=wax[gi][:, cc, :], rhs=xb[:, tc * CH:(tc + 1) * CH], start=True, stop=True),
                              reads=[wax[gi], xb], writes=[ps])
                        tk.op("act", lambda e, ps=ps, dst=dst, bcol=bcol, tc=tc: e.activation(out=dst[:, tc * CH:(tc + 1) * CH], in_=ps[:, 0:CH], func=AF.Sigmoid,
                                                                                         bias=vecs[:, vb + bcol + cc:vb + bcol + cc + 1]), reads=[ps, vecs], writes=[dst], waw=False)
                a_, x2, w_, t_ = P["a"](), P["x2"](), P["w"](), P["t"]()
                tk.op("act", lambda e: e.activation(out=a_[:], in_=r_[:], func=AF.Exp, scale=spc[:, cc:cc + 1]), reads=[r_, spc], writes=[a_])
                tk.op("dve", lambda e: e.tensor_scalar(out=x2[:], in0=r_[:], scalar1=spc[:, cc:cc + 1], scalar2=2.0, op0=ALU.mult, op1=ALU.mult), reads=[r_, spc], writes=[x2])
                tk.op("dve", lambda e: e.tensor_scalar(out=w_[:], in0=x2[:], scalar1=1.0 / 6.0, scalar2=None, op0=ALU.mult), reads=[x2], writes=[w_])
                for k in (5, 4, 3, 2):
                    tk.op("act", lambda e, k=k: e.activation(out=t_[:], in_=w_[:], func=AF.Identity, scale=1.0 / k, bias=self.kb[:, k:k + 1]), reads=[w_, self.kb], writes=[t_])
                    tk.op("dve", lambda e: e.tensor_tensor(out=w_[:], in0=t_[:], in1=x2[:], op=ALU.mult), reads=[t_, x2], writes=[w_])
                tk.op("dve", lambda e: e.scalar_tensor_tensor(out=w_[:], in0=w_[:], scalar=1.0, in1=x2[:], op0=ALU.add, op1=ALU.mult), reads=[w_, x2], writes=[w_])
                tk.op("act", lambda e: e.activation(out=t_[:], in_=w_[:], func=AF.Sqrt, scale=-1.0), reads=[w_], writes=[t_])
                tk.op("dve", lambda e: e.tensor_tensor(out=t_[:], in0=t_[:], in1=ig[:], op=ALU.mult), reads=[t_, ig], writes=[t_])
                tk.op("dve", lambda e: e.tensor_tensor(out=t_[:], in0=t_[:], in1=xr[:], op=ALU.mult), reads=[t_, xr], writes=[t_])
                h_, cp = P["h"](), P["cp"]()
                tk.op("dve", lambda e: e.tensor_tensor_scan(out=h_[:], data0=a_[:], data1=t_[:], initial=0.0, op0=ALU.mult, op1=ALU.add), reads=[a_, t_], writes=[h_])
                tk.op("dve", lambda e: e.tensor_tensor_scan(out=cp[:], data0=a_[:], data1=zer[:], initial=1.0, op0=ALU.mult, op1=ALU.add), reads=[a_, zer], writes=[cp])
                tk.op("act", lambda e: e.activation(out=car[:, cc:cc + 1], in_=cp[:, T - 1:T], func=AF.Copy), reads=[cp], writes=[car], waw=False)
                tk.op("act", lambda e: e.activation(out=car[:, 16 + cc:17 + cc], in_=h_[:, T - 1:T], func=AF.Copy), reads=[h_], writes=[car], waw=False)
                tk.dma("pool", self.HLOC.ap[cc * 128:(cc + 1) * 128, :], h_[:], reads=[h_], writes=[self.HLOC])
                tk.dma("pool", self.CPT.ap[cc * 128:(cc + 1) * 128, :], cp[:], reads=[cp], writes=[self.CPT])
            tk.dma("sp", self.CARI.ap, car[:], reads=[car], writes=[self.CARI])
            tk.allgather(self.CARO.ap.opt(), self.CARI.ap.opt(), reads=[self.CARI], writes=[self.CARO])
            call = self.sb(es, "call", [128, NCORES, 32])
            tk.dma("sp", call[:], self.CARO.ap.rearrange("(r p) n -> p r n", p=128), reads=[self.CARO], writes=[call])
            Hin = self.sb(es, "Hin", [128, 16])
            tmp = self.sb(es, "ctmp", [128, 16])
            tk.op("pool", lambda e: e.memset(Hin[:], 0.0), writes=[Hin])
            for r in range(NCORES - 1):
                tk.op("dve", lambda e, r=r: e.tensor_tensor(out=tmp[:], in0=call[:, r, 0:16], in1=Hin[:], op=ALU.mult), reads=[call, Hin], writes=[tmp])
                tk.op("dve", lambda e, r=r: e.tensor_tensor(out=tmp[:], in0=tmp[:], in1=call[:, r, 16:32], op=ALU.add), reads=[tmp, call], writes=[tmp])
                tk.op("dve", lambda e: e.tensor_tensor(out=tmp[:], in0=tmp[:], in1=Hin[:], op=ALU.subtract), reads=[tmp, Hin], writes=[tmp])
                tk.op("dve", lambda e, r=r: e.scalar_tensor_tensor(out=Hin[:], in0=tmp[:], scalar=flg[:, r:r + 1], in1=Hin[:], op0=ALU.mult, op1=ALU.add),
                      reads=[tmp, flg, Hin], writes=[Hin])
            ob = self.pool(es, "rob", [128, T], BF16, 2)
            for cc in range(16):
                h_, cp, yg = P["h"](), P["cp"](), P["t"]()
                tk.dma("sp", h_[:], self.HLOC.ap[cc * 128:(cc + 1) * 128, :], reads=[self.HLOC], writes=[h_])
                tk.dma("sp", cp[:], self.CPT.ap[cc * 128:(cc + 1) * 128, :], reads=[self.CPT], writes=[cp])
                tk.dma("sp", yg[:], self.RYG.ap[cc * 128:(cc + 1) * 128, :], reads=[self.RYG], writes=[yg])
                tk.op("dve", lambda e: e.scalar_tensor_tensor(out=h_[:], in0=cp[:], scalar=Hin[:, cc:cc + 1], in1=h_[:], op0=ALU.mult, op1=ALU.add),
                      reads=[cp, Hin, h_], writes=[h_])
                o = ob()
                tk.op("dve", lambda e: e.tensor_tensor(out=o[:], in0=h_[:], in1=yg[:], op=ALU.mult), reads=[h_, yg], writes=[o])
                tk.dma("pool", self.RNNT.ap[cc * 128:(cc + 1) * 128, :], o[:], reads=[o], writes=[self.RNNT])
            tk.barrier()
        if l == 0:
            self.tap("RNNT", self.RNNT, self.RNNT.ap, [2048, T], BF16)

    def phaseC(self, l):
        nc, tk = self.nc, self.tk
        S, T, NQ, CH, NCH = self.S, self.T, self.NQ, self.CH, self.NCH
        if l == 0:
            self.MRG = self.dint("MRG", [D, T], BF16)
        WA = self.WB["w_attn_up"][l][1]
        WR = self.WB["w_rnn_up"][l][1]
        WO = self.WB["w_out"][l][1]
        with contextlib.ExitStack() as es:
            aT = self.sb(es, "aT", [128, 16, T], BF16)
            rT = self.sb(es, "rT", [128, 16, T], BF16)
            tk.dma("sp", aT[:], self.ATT.ap.rearrange("(k p) t -> p k t", p=128), reads=[self.ATT], writes=[aT])
            tk.dma("sp", rT[:], self.RNNT.ap.rearrange("(k p) t -> p k t", p=128), reads=[self.RNNT], writes=[rT])
            pa_p = self.pool(es, "pa", [128, 16, 256], BF16, 2)
            pr_p = self.pool(es, "pr", [128, 16, 256], BF16, 2)
            sg_p = self.pool(es, "sg", [128, CH], F32, 4)
            t_p = self.pool(es, "ct", [128, CH], F32, 4)
            o_p = self.pool(es, "co", [128, CH], BF16, 3)
            WAv = WA.ap.rearrange("(k p) n -> p k n", p=128)
            WRv = WR.ap.rearrange("(k p) n -> p k n", p=128)
            for o2 in range(16):
                pa, pr = pa_p(), pr_p()
                tk.dma("sp", pa[:], WAv[:, :, o2 * 256:(o2 + 1) * 256], reads=[WA], writes=[pa])
                tk.dma("sp", pr[:], WRv[:, :, o2 * 256:(o2 + 1) * 256], reads=[WR], writes=[pr])
                for j in range(2):
                    oc = o2 * 2 + j
                    for tc in range(NCH):
                        sl = slice(tc * CH, (tc + 1) * CH)
                        ps1, ps2 = self.ps(), self.ps()
                        for kc in range(16):
                            tk.op("pe", lambda e, kc=kc, ps1=ps1: e.matmul(ps1[:, 0:CH], lhsT=pa[:, kc, j * 128:(j + 1) * 128], rhs=aT[:, kc, sl], start=(kc == 0), stop=(kc == 15)),
                                  reads=[pa, aT], writes=[ps1], waw=(kc == 0))
                        for kc in range(16):
                            tk.op("pe", lambda e, kc=kc, ps2=ps2: e.matmul(ps2[:, 0:CH], lhsT=pr[:, kc, j * 128:(j + 1) * 128], rhs=rT[:, kc, sl], start=(kc == 0), stop=(kc == 15)),
                                  reads=[pr, rT], writes=[ps2], waw=(kc == 0))
                        sa, sr = sg_p(), sg_p()
                        tk.dma("sp", sa[:], self.SGA.ap[oc * 128:(oc + 1) * 128, sl], reads=[self.SGA], writes=[sa])
                        tk.dma("sp", sr[:], self.SGR.ap[oc * 128:(oc + 1) * 128, sl], reads=[self.SGR], writes=[sr])
                        t1, t2 = t_p(), t_p()
                        tk.op("dve", lambda e: e.tensor_tensor(out=t1[:], in0=ps1[:, 0:CH], in1=sa[:], op=ALU.mult), reads=[ps1, sa], writes=[t1])
                        tk.op("dve", lambda e: e.tensor_tensor(out=t2[:], in0=ps2[:, 0:CH], in1=sr[:], op=ALU.mult), reads=[ps2, sr], writes=[t2])
                        o = o_p()
                        tk.op("dve", lambda e: e.tensor_tensor(out=o[:], in0=t1[:], in1=t2[:], op=ALU.add), reads=[t1, t2], writes=[o])
                        tk.dma("pool", self.MRG.ap[oc * 128:(oc + 1) * 128, sl], o[:], reads=[o], writes=[self.MRG])
            tk.barrier()
        if l == 0:
            self.tap("MRG", self.MRG, self.MRG.ap, [D, T], BF16)
        self.proj_residual(l, self.MRG, KC, WO, 64)

    def proj_residual(self, l, SRC, nkc, W, gofs):
        tk = self.tk
        T, CH, NCH = self.T, self.CH, self.NCH
        Wv = W.ap.rearrange("(k p) n -> p k n", p=128)
        with contextlib.ExitStack() as es:
            src = self.sb(es, "pjs", [128, nkc, CH], BF16)
            wp = self.pool(es, "pjw", [128, nkc, 128], BF16, 2)
            hp = self.pool(es, "pjh", [128, CH], F32, 3)
            for tc in range(NCH):
                sl = slice(tc * CH, (tc + 1) * CH)
                for k4 in range(0, nkc, 16):
                    tk.dma("sp", src[:, k4:k4 + 16, :], SRC.ap[k4 * 128:(k4 + 16) * 128, sl].rearrange("(k p) t -> p k t", p=128), reads=[SRC], writes=[src])
                for oc in range(KC):
                    w = wp()
                    for k4 in range(0, nkc, 16):
                        tk.dma("sp", w[:, k4:k4 + 16, :], Wv[:, k4:k4 + 16, oc * 128:(oc + 1) * 128], reads=[W], writes=[w])
                    ps = self.ps()
                    for kc in range(nkc):
                        tk.op("pe", lambda e, kc=kc, ps=ps, w=w: e.matmul(ps[:, 0:CH], lhsT=w[:, kc, :], rhs=src[:, kc, :], start=(kc == 0), stop=(kc == nkc - 1)),
                              reads=[w, src], writes=[ps], waw=(kc == 0))
                    h = hp()
                    tk.dma("sp", h[:], self.HT.ap[oc * 128:(oc + 1) * 128, sl], reads=[self.HTc[tc]], writes=[h])
                    tk.op("dve", lambda e, ps=ps, h=h, oc=oc: e.scalar_tensor_tensor(out=h[:], in0=ps[:, 0:CH], scalar=self.modc[:, l, gofs + oc:gofs + oc + 1], in1=h[:],
                                                                                 op0=ALU.mult, op1=ALU.add), reads=[ps, self.modc, h], writes=[h])
                    tk.dma("pool", self.HT.ap[oc * 128:(oc + 1) * 128, sl], h[:], reads=[h], writes=[self.HTc[tc]])
            tk.barrier()

    def phaseD(self, l):
        nc, tk = self.nc, self.tk
        S, T, NQ, CH, NCH = self.S, self.T, self.NQ, self.CH, self.NCH
        cst, vecs = self.cst, self.vecs
        vb = l * NV
        if l == 0:
            self.ACTT = self.dint("ACTT", [DFF, T], BF16)
            self.U2I = self.dint("U2I", [128, KC * 2], BF16)
            self.U2O = self.dint("U2O", [NCORES * 128, KC * 2], BF16)
        WF = self.WB["w_ffn_in"][l][1]
        WD = self.WB["w_ffn_down"][l][1]
        WFv = WF.ap.rearrange("(k p) n -> p k n", p=128)
        with contextlib.ExitStack() as es:
            uT = self.sb(es, "u2T", [128, KC, T], BF16)
            with contextlib.ExitStack() as es2:
                self.norm_to_uT(es2, l, 1, uT)
                tk.barrier()
            uh = self.sb(es, "u2h", [128, KC, 2], BF16)
            uh2 = self.sb(es, "u2h2", [128, KC, 2], BF16)
            tk.dma("sp", self.U2I.ap.rearrange("p (k j) -> p k j", j=2), uT[:, :, T - 2:T], reads=[uT], writes=[self.U2I])
            tk.allgather(self.U2O.ap.opt(), self.U2I.ap.opt(), reads=[self.U2I], writes=[self.U2O])
            prev = (self.pid + (NCORES - 1)) % NCORES
            tk.dma("sp", uh[:], self.U2O.ap[bass.ds(prev * 128, 128), :].rearrange("p (k j) -> p k j", j=2), reads=[self.U2O], writes=[uh])
            tk.op("dve", lambda e: e.tensor_scalar(out=uh2[:], in0=uh[:], scalar1=cst[:, 513:514], scalar2=None, op0=ALU.mult), reads=[uh, cst], writes=[uh2])
            pg_p = self.pool(es, "pg", [128, KC, 256], BF16, 2)
            pu_p = self.pool(es, "pu", [128, KC, 256], BF16, 2)
            G_p = self.pool(es, "fG", [128, T + 2], F32, 2)
            c_p = self.pool(es, "fc", [128, T], F32, 2)
            ab_p = self.pool(es, "fab", [128, T], BF16, 2)
            for f2 in range(DFF // 256):
                pg, pu = pg_p(), pu_p()
                for q4 in range(4):
                    tk.dma("sp", pg[:, q4 * 8:(q4 + 1) * 8, :], WFv[:, q4 * 8:(q4 + 1) * 8, f2 * 256:(f2 + 1) * 256], reads=[WF], writes=[pg])
                    tk.dma("sp", pu[:, q4 * 8:(q4 + 1) * 8, :], WFv[:, q4 * 8:(q4 + 1) * 8, DFF + f2 * 256:DFF + (f2 + 1) * 256], reads=[WF], writes=[pu])
                for j in range(2):
                    fc = f2 * 2 + j
                    G = G_p()
                    ps = self.ps()
                    for kc in range(KC):
                        tk.op("pe", lambda e, kc=kc, ps=ps: e.matmul(ps[:, 0:2], lhsT=pg[:, kc, j * 128:(j + 1) * 128], rhs=uh2[:, kc, :], start=(kc == 0), stop=(kc == KC - 1)),
                              reads=[pg, uh2], writes=[ps], waw=(kc == 0))
                    tk.op("act", lambda e, ps=ps: e.activation(out=G[:, 0:2], in_=ps[:, 0:2], func=AF.Copy), reads=[ps], writes=[G])
                    for tc in range(NCH):
                        ps = self.ps()
                        for kc in range(KC):
                            tk.op("pe", lambda e, kc=kc, ps=ps, tc=tc: e.matmul(ps[:, 0:CH], lhsT=pg[:, kc, j * 128:(j + 1) * 128], rhs=uT[:, kc, tc * CH:(tc + 1) * CH],
                                                                              start=(kc == 0), stop=(kc == KC - 1)), reads=[pg, uT], writes=[ps], waw=(kc == 0))
                        tk.op("act", lambda e, ps=ps, tc=tc: e.activation(out=G[:, 2 + tc * CH:2 + (tc + 1) * CH], in_=ps[:, 0:CH], func=AF.Copy), reads=[ps], writes=[G], waw=False)
                    ups = []
                    for tc in range(NCH):
                        ps = self.ps()
                        for kc in range(KC):
                            tk.op("pe", lambda e, kc=kc, ps=ps, tc=tc: e.matmul(ps[:, 0:CH], lhsT=pu[:, kc, j * 128:(j + 1) * 128], rhs=uT[:, kc, tc * CH:(tc + 1) * CH],
                                                                              start=(kc == 0), stop=(kc == KC - 1)), reads=[pu, uT], writes=[ps], waw=(kc == 0))
                        ups.append(ps)
                    c = c_p()
                    fw = lambda jj: vecs[:, vb + V_FCW + jj * 64 + fc:vb + V_FCW + jj * 64 + fc + 1]
                    tk.op("dve", lambda e: e.tensor_scalar(out=c[:], in0=G[:, 0:T], scalar1=fw(0), scalar2=vecs[:, vb + V_FCB + fc:vb + V_FCB + fc + 1], op0=ALU.mult, op1=ALU.add),
                          reads=[G, vecs], writes=[c])
                    for jj in (1, 2):
                        tk.op("dve", lambda e, jj=jj: e.scalar_tensor_tensor(out=c[:], in0=G[:, jj:jj + T], scalar=fw(jj), in1=c[:], op0=ALU.mult, op1=ALU.add),
                              reads=[G, vecs, c], writes=[c])
                    tk.op("act", lambda e: e.activation(out=c[:], in_=c[:], func=AF.Silu), reads=[c], writes=[c])
                    ab = ab_p()
                    for tc in range(NCH):
                        tk.op("dve", lambda e, tc=tc: e.tensor_tensor(out=ab[:, tc * CH:(tc + 1) * CH], in0=c[:, tc * CH:(tc + 1) * CH], in1=ups[tc][:, 0:CH], op=ALU.mult),
                              reads=[c, ups[tc]], writes=[ab], waw=False)
                    tk.dma("pool", self.ACTT.ap[fc * 128:(fc + 1) * 128, :], ab[:], reads=[ab], writes=[self.ACTT])
            tk.barrier()
        self.proj_residual(l, self.ACTT, DFF // 128, WD, 160)

    def gelu(self, ps, of, wk):
        tk, CH = self.tk, self.CH
        t1 = wk[0]()
        tk.op("act", lambda e: e.activation(out=t1[:], in_=ps[:, 0:CH], func=AF.Square), reads=[ps], writes=[t1])
        tk.op("dve", lambda e: e.tensor_scalar(out=t1[:], in0=t1[:], scalar1=0.044715, scalar2=1.0, op0=ALU.mult, op1=ALU.add), reads=[t1], writes=[t1])
        tk.op("dve", lambda e: e.tensor_tensor(out=t1[:], in0=t1[:], in1=ps[:, 0:CH], op=ALU.mult), reads=[t1, ps], writes=[t1])
        tk.op("act", lambda e: e.activation(out=t1[:], in_=t1[:], func=AF.Sigmoid, scale=1.5957691216057308), reads=[t1], writes=[t1])
        tk.op("dve", lambda e: e.tensor_tensor(out=of[:], in0=t1[:], in1=ps[:, 0:CH], op=ALU.mult), reads=[t1, ps], writes=[of])


def _pp(v, n):
    return np.ascontiguousarray(np.asarray(v, np.float32).reshape(n, 128).T)


def make_tables(b, c):
    S, T, NQ, NKT, NCMP, NCT, NSEL = b.S, b.T, b.NQ, b.NKT, b.NCMP, b.NCT, b.NSEL
    tb = {}
    cmpb = np.zeros((NQ, 128, NCT, 4, 128), np.float32)
    ovt = np.zeros((NQ, 128, NCT, 128), np.float32)
    sel = np.zeros((NQ, 128, 3, 128), np.float32)
    wb = np.zeros((NQ, 5, 512), np.float32)
    n_glob = (np.arange(NCT)[None, :] * 128 + np.arange(128)[:, None])
    cmp_start = np.arange(NCMP) * 16
    sel_start = np.arange(NSEL) * 64
    ov = np.clip(np.minimum(cmp_start[:, None] + 32, sel_start[None, :] + 64) - np.maximum(cmp_start[:, None], sel_start[None, :]), 0, None) / 32.0
    for i in range(NQ):
        Q = c * NQ + i
        t = Q * 128 + np.arange(128)
        masked = (n_glob[:, :, None] >= NCMP) | (16 * n_glob[:, :, None] + 31 > t[None, None, :])
        cmpb[i] = np.where(masked, -BIG, 0.0)[:, :, None, :]
        j_of = 2 * Q + 1 - np.arange(128)
        okj = (j_of >= 0) & (j_of < NSEL)
        for nt in range(NCT):
            for n in range(128):
                ng = nt * 128 + n
                if ng < NCMP:
                    ovt[i, n, nt, okj] = ov[ng, j_of[okj]]
        cur = t // 64
        valid = okj[None, :] & (j_of[None, :] <= cur[:, None])
        forced = valid & ((j_of[None, :] == 0) | (j_of[None, :] > cur[:, None] - 2))
        sel[i, :, 0, :] = (valid & ~forced)
        sel[i, :, 1, :] = np.where(forced, 9.0, np.where(valid, 0.0, -1.0))
        sel[i, :, 2, :] = valid
        for r in range(5):
            if Q - r < 0:
                wb[i, r, :] = -BIG
    tb["tb_cmpb"] = cmpb.reshape(NQ * 128, NCT * 512)
    tb["tb_ov"] = ovt.reshape(NQ * 128, NCT * 128)
    tb["tb_sel"] = sel.reshape(NQ * 128, 3 * 128)
    tb["tb_wb"] = wb.reshape(1, NQ * 5 * 512)
    E = np.zeros((128, NKT, 128), np.float32)
    for r in range(NKT):
        if 2 * r + 1 < 128:
            E[2 * r + 1, r, 0:64] = 1.0
        if 2 * r < 128:
            E[2 * r, r, 64:128] = 1.0
    tb["tb_e"] = E.reshape(128, NKT * 128)
    p = np.arange(128)[:, None]
    q = np.arange(128)[None, :]
    caus = np.zeros((128, 2, 4, 128), np.float32)
    caus[:, 0] = np.where(p > q, -BIG, 0.0)[:, None, :]
    caus[:, 1] = np.where(p <= q, -BIG, 0.0)[:, None, :]
    tb["tb_caus"] = caus.reshape(128, 1024)
    return tb


def make_inputs(b, inp):
    S, T, DEPTH = b.S, b.T, b.DEPTH
    consts = np.zeros((128, 4 * 128 + 2), np.float32)
    consts[:, 0:128] = np.eye(128, dtype=np.float32)
    consts[:, 128:256] = 1.0
    rm = np.zeros((128, 128), np.float32)
    for d in range(16):
        rm[d + 16, d] = -1.0
        rm[d, d + 16] = 1.0
    consts[:, 256:384] = rm
    invf = (500000.0 ** (-np.arange(0, 32, 2, dtype=np.float32) / 32)).astype(np.float32)
    for p in range(32):
        consts[p, 512] = invf[p % 16]
    vec_l = []
    for l in range(DEPTH):
        cols = [_pp(inp["norm_mix"][l], 32), _pp(inp["norm_ffn"][l], 32), _pp(inp["q_norm"][l], 1)]
        cols += [_pp(inp["k_norm"][l][i], 1) for i in range(3)]
        cols += [_pp(inp["rnn_conv_w"][l][j], 16) for j in range(4)]
        cols += [_pp(inp["rnn_conv_b"][l], 16), _pp(inp["rg_b_a"][l], 16), _pp(inp["rg_b_x"][l], 16), _pp(inp["rg_lambda"][l], 16)]
        cols += [_pp(inp["ffn_conv_w"][l][j], 64) for j in range(3)]
        cols += [_pp(inp["ffn_conv_b"][l], 64)]
        vec_l.append(np.concatenate(cols, axis=1))
    vecs = np.ascontiguousarray(np.concatenate(vec_l, axis=1))
    assert vecs.shape[1] == DEPTH * NV
    maps = []
    for c in range(NCORES):
        m = {}
        m["x"] = np.ascontiguousarray(inp["x"][0, c * T:(c + 1) * T, :])
        m["pos"] = np.ascontiguousarray(inp["positions"][:, c * T:(c + 1) * T]).astype(np.int32)
        m["cT"] = _pp(inp["c"][0], 32)
        m["w_cond"] = np.ascontiguousarray(inp["w_cond"])
        m["b_condT"] = _pp(inp["b_cond"], 4)
        m["w_mod"] = np.ascontiguousarray(inp["w_mod"][:, :, c * 3072:(c + 1) * 3072]).reshape(DEPTH * 512, 3072)
        m["b_modT"] = np.ascontiguousarray(np.concatenate([_pp(inp["b_mod"][l, c * 3072:(c + 1) * 3072], 24) for l in range(DEPTH)], axis=1))
        m["vecs"] = vecs
        cc = consts.copy()
        cc[:, 513] = 0.0 if c == 0 else 1.0
        m["consts"] = cc
        m.update(make_tables(b, c))
        m["cmp_w_k"] = np.ascontiguousarray(inp["cmp_w_k"]).reshape(DEPTH * 4096, 128)
        m["cmp_w_v"] = np.ascontiguousarray(inp["cmp_w_v"]).reshape(DEPTH * 4096, 128)
        m["cmp_peT"] = np.ascontiguousarray(np.concatenate(
            [np.concatenate([inp["cmp_pe_k"][l].T, inp["cmp_pe_v"][l].T], axis=1) for l in range(DEPTH)], axis=1)).astype(np.float32)
        m["k_norm0"] = np.ascontiguousarray(inp["k_norm"][:, 0, :])
        m["rg_w_a"] = np.ascontiguousarray(inp["rg_w_a"]).reshape(DEPTH * 2048, 128)
        m["rg_w_x"] = np.ascontiguousarray(inp["rg_w_x"]).reshape(DEPTH * 2048, 128)
        fl = np.zeros((128, NCORES), np.float32)
        fl[:, :c] = 1.0
        m["flags"] = fl
        for nm in ("w_in", "w_ffn_in", "w_ffn_down", "w_out", "w_attn_up", "w_rnn_up"):
            w = inp[nm]
            rs = w.shape[1] // NCORES
            m[nm] = np.ascontiguousarray(w[:, c * rs:(c + 1) * rs, :]).reshape(DEPTH * rs, w.shape[2])
        maps.append(m)
    return maps


def run(inp, S, DEPTH, dbg=()):
    b = Builder(S, DEPTH, dbg)
    nc = b.build()
    maps = make_inputs(b, inp)
    maps = [{k: v for k, v in m.items() if k in b.ins} for m in maps]
    res = run_bass_kernel_spmd(nc, maps, core_ids=list(range(NCORES)))
    return b, res.results


def kernel(**inputs):
    inp = {k: np.asarray(v) for k, v in inputs.items()}
    try:
        b, results = run(inp, 8192, 4)
    except Exception as ex:
        if "Internal" not in str(ex) and "UNAVAILABLE" not in str(ex) and "INTERNAL" not in str(ex):
            raise
        import time
        time.sleep(45)
        b, results = run(inp, 8192, 4)
    y = np.concatenate([results[c]["y"] for c in range(NCORES)], axis=0)
    return y.reshape(1, 8192, D).astype(np.float32)
```
